# Optimizing a Trainium2 kernel written in Bass

```python
import math
import jax, jax.numpy as jnp
from jax import lax
import numpy as np

D_MODEL = 1024
BATCH = 8
SEQ = 2048
DEPTH = 4
DEC_BATCH = 128
DEC_SEQ = 8
PAST_LEN = 16384
PAGE_SIZE = 128

N_BRANCH = 4
MIX_W = D_MODEL // 4
POOL_WINDOWS = (2, 4, 8, 16)
POOL_GROUPS = len(POOL_WINDOWS)
POOL_GW = MIX_W // POOL_GROUPS
POOL_BUF = max(POOL_WINDOWS) - 1
SSM_GROUP_CH = 16
SSM_GROUPS = MIX_W // SSM_GROUP_CH
SSM_STATE = 64
SC_WIDTH = 3
CF_WIDTH = 31
FF_DIM = -(-8 * D_MODEL // (3 * 256)) * 256
OFF_POOL = 0
OFF_SSM = OFF_POOL + MIX_W
OFF_SC = OFF_SSM + MIX_W
OFF_CF = OFF_SC + 3 * MIX_W
OFF_GATE = OFF_CF + 2 * MIX_W
IN_COLS = OFF_GATE + N_BRANCH * D_MODEL
RMS_EPS = 1e-6
LN_EPS = 1e-5

kernel_name = 'hybrid_pool_s5_conv_gated_decoder_step'

f32 = jnp.float32


def rmsnorm(x, g):
    xf = x.astype(f32)
    y = xf * lax.rsqrt(jnp.mean(xf * xf, axis=-1, keepdims=True) + RMS_EPS)
    return (y * g.astype(f32)).astype(x.dtype)


def causal_dwconv(v, buf, w):
    ext = jnp.concatenate([buf.astype(v.dtype), v], axis=1)
    y = lax.conv_general_dilated(ext, w[:, None, :].astype(v.dtype), window_strides=(1,), padding='VALID',
                                 dimension_numbers=('NWC', 'WIO', 'NWC'), feature_group_count=v.shape[-1])
    return y, ext[:, ext.shape[1] - (w.shape[0] - 1):]


def pool_mixer(v, buf, pos0, w, scale):
    L = v.shape[1]
    ext = jnp.concatenate([buf.astype(v.dtype), v], axis=1)
    extf = ext.astype(f32)
    cs = jnp.concatenate([jnp.zeros_like(extf[:, :1]), jnp.cumsum(extf, axis=1)], axis=1)
    pos = jnp.arange(L, dtype=f32) + pos0
    outs = []
    for k, win in enumerate(POOL_WINDOWS):
        lo, hi = k * POOL_GW, (k + 1) * POOL_GW
        s = cs[:, POOL_BUF + 1:POOL_BUF + 1 + L, lo:hi] - cs[:, POOL_BUF + 1 - win:POOL_BUF + 1 - win + L, lo:hi]
        cnt = jnp.minimum(pos + 1.0, float(win))
        d = s / cnt[None, :, None] - extf[:, POOL_BUF:, lo:hi]
        outs.append(jnp.einsum('nlc,cd->nld', d, w[k].astype(f32)))
    out = jnp.concatenate(outs, axis=-1) * scale.astype(f32)
    return out.astype(v.dtype), ext[:, ext.shape[1] - POOL_BUF:]


def _ssm_combine(e1, e2):
    a1, b1 = e1
    a2, b2 = e2
    return a1 * a2, a2 * b1 + b2


def ssm_mixer(u, h_re, h_im, lam_re, lam_im, log_dt, b_re, b_im, c_re, c_im, d_skip, w_glu, b_glu):
    N, L, _ = u.shape
    uf = u.astype(f32).reshape(N, L, SSM_GROUPS, SSM_GROUP_CH)
    lam = lam_re.astype(f32) + 1j * lam_im.astype(f32)
    dt = jnp.exp(log_dt.astype(f32))[:, None]
    lam_bar = jnp.exp(lam * dt)
    b_bar = ((lam_bar - 1.0) / lam)[..., None] * (b_re.astype(f32) + 1j * b_im.astype(f32))
    c = c_re.astype(f32) + 1j * c_im.astype(f32)
    bu = jnp.einsum('nlgc,gpc->nlgp', uf.astype(jnp.complex64), b_bar)
    h0 = h_re.astype(f32) + 1j * h_im.astype(f32)
    bu = bu.at[:, 0].add(lam_bar[None] * h0)
    a = jnp.broadcast_to(lam_bar, (1, L) + lam_bar.shape)
    _, h = lax.associative_scan(_ssm_combine, (a, bu), axis=1)
    y = jnp.einsum('nlgp,gcp->nlgc', h, c).real + d_skip.astype(f32).reshape(SSM_GROUPS, SSM_GROUP_CH) * uf
    z = jax.nn.gelu(y.reshape(N, L, MIX_W))
    out = z * jax.nn.sigmoid(z @ w_glu.astype(f32) + b_glu.astype(f32))
    h_last = h[:, -1]
    return out.astype(u.dtype), h_last.real, h_last.imag


def shortconv_mixer(bg, cg, hx, buf, w):
    y, nb = causal_dwconv(cg * hx, buf, w)
    return bg * y, nb


def conformer_mixer(a, b, buf, w, ln_g, ln_b):
    v = a * jax.nn.sigmoid(b)
    y, nb = causal_dwconv(v, buf, w)
    yf = y.astype(f32)
    mu = jnp.mean(yf, axis=-1, keepdims=True)
    var = jnp.mean(jnp.square(yf - mu), axis=-1, keepdims=True)
    yn = (yf - mu) * lax.rsqrt(var + LN_EPS) * ln_g.astype(f32) + ln_b.astype(f32)
    return jax.nn.silu(yn).astype(v.dtype), nb


def trunk(x, pos0, st_pool, st_re, st_im, st_sc, st_cf, weights):
    (norm1_g, norm2_g, final_g, w_in, b_gate, pool_w, pool_scale, lam_re, lam_im, log_dt,
     b_re, b_im, c_re, c_im, d_skip, w_glu, b_glu, sc_w, cf_w, cf_ln_g, cf_ln_b,
     w_branch, w_out, w_ffn_in, w_ffn_out) = weights
    N, L = x.shape[0], x.shape[1]
    n_pool, n_re, n_im, n_sc, n_cf = [], [], [], [], []
    for i in range(DEPTH):
        h = rmsnorm(x, norm1_g[i])
        proj = jnp.einsum('nld,de->nle', h, w_in[i])
        ya, pb = pool_mixer(proj[..., OFF_POOL:OFF_SSM], st_pool[i], pos0, pool_w[i], pool_scale[i])
        yb, hr, hi = ssm_mixer(proj[..., OFF_SSM:OFF_SC], st_re[i], st_im[i], lam_re[i], lam_im[i], log_dt[i],
                               b_re[i], b_im[i], c_re[i], c_im[i], d_skip[i], w_glu[i], b_glu[i])
        yc, sb = shortconv_mixer(proj[..., OFF_SC:OFF_SC + MIX_W], proj[..., OFF_SC + MIX_W:OFF_SC + 2 * MIX_W],
                                 proj[..., OFF_SC + 2 * MIX_W:OFF_CF], st_sc[i], sc_w[i])
        yd, cb = conformer_mixer(proj[..., OFF_CF:OFF_CF + MIX_W], proj[..., OFF_CF + MIX_W:OFF_GATE],
                                 st_cf[i], cf_w[i], cf_ln_g[i], cf_ln_b[i])
        gates = jax.nn.sigmoid(proj[..., OFF_GATE:].reshape(N, L, N_BRANCH, D_MODEL) + b_gate[i])
        branches = jnp.stack([ya, yb, yc, yd], axis=2)
        bproj = jnp.einsum('nlkc,kcd->nlkd', branches, w_branch[i])
        merged = jnp.sum(gates * bproj, axis=2)
        x = x + merged @ w_out[i]
        h2 = rmsnorm(x, norm2_g[i])
        g, up = jnp.split(h2 @ w_ffn_in[i], 2, axis=-1)
        x = x + (jax.nn.silu(g) * up) @ w_ffn_out[i]
        n_pool.append(pb); n_re.append(hr); n_im.append(hi); n_sc.append(sb); n_cf.append(cb)
    return (rmsnorm(x, final_g), jnp.stack(n_pool), jnp.stack(n_re), jnp.stack(n_im),
            jnp.stack(n_sc), jnp.stack(n_cf))


def setup_inputs(seed: int = 0) -> dict:
    key = jax.random.key(seed)
    ks = jax.random.split(key, 40)
    nrm = lambda k, shape, s: jax.random.normal(k, shape, f32) * s
    D, C = D_MODEL, MIX_W
    G, P, GC = SSM_GROUPS, SSM_STATE, SSM_GROUP_CH
    lam_im0 = jnp.broadcast_to(math.pi * jnp.arange(P, dtype=f32), (DEPTH, G, P))
    return {
        'x_prompt': nrm(ks[0], (BATCH, SEQ, D), 1.0),
        'x_sample': nrm(ks[1], (DEC_BATCH, DEC_SEQ, D), 1.0),
        'state_pool': nrm(ks[2], (DEPTH, DEC_BATCH, POOL_BUF, C), 1.0),
        'state_ssm_re': nrm(ks[3], (DEPTH, DEC_BATCH, G, P), 0.5),
        'state_ssm_im': nrm(ks[4], (DEPTH, DEC_BATCH, G, P), 0.5),
        'state_shortconv': nrm(ks[5], (DEPTH, DEC_BATCH, SC_WIDTH - 1, C), 1.0),
        'state_conformer': nrm(ks[6], (DEPTH, DEC_BATCH, CF_WIDTH - 1, C), 0.5),
        'norm1_g': 1.0 + nrm(ks[7], (DEPTH, D), 0.01),
        'norm2_g': 1.0 + nrm(ks[8], (DEPTH, D), 0.01),
        'final_g': 1.0 + nrm(ks[9], (D,), 0.01),
        'w_in': nrm(ks[10], (DEPTH, D, IN_COLS), D ** -0.5),
        'b_gate': nrm(ks[11], (DEPTH, N_BRANCH, D), 0.01),
        'pool_w': nrm(ks[12], (DEPTH, POOL_GROUPS, POOL_GW, POOL_GW), POOL_GW ** -0.5),
        'pool_scale': 1.0 + nrm(ks[13], (DEPTH, C), 0.1),
        'lam_re': -0.5 + nrm(ks[14], (DEPTH, G, P), 0.01),
        'lam_im': lam_im0 + nrm(ks[15], (DEPTH, G, P), 0.01),
        'log_dt': jax.random.uniform(ks[16], (DEPTH, G), f32, math.log(1e-3), math.log(1e-1)),
        'b_re': nrm(ks[17], (DEPTH, G, P, GC), (2 * GC) ** -0.5),
        'b_im': nrm(ks[18], (DEPTH, G, P, GC), (2 * GC) ** -0.5),
        'c_re': nrm(ks[19], (DEPTH, G, GC, P), (2 * P) ** -0.5),
        'c_im': nrm(ks[20], (DEPTH, G, GC, P), (2 * P) ** -0.5),
        'd_skip': nrm(ks[21], (DEPTH, C), 1.0),
        'w_glu': nrm(ks[22], (DEPTH, C, C), C ** -0.5),
        'b_glu': nrm(ks[23], (DEPTH, C), 0.01),
        'sc_w': nrm(ks[24], (DEPTH, SC_WIDTH, C), SC_WIDTH ** -0.5),
        'cf_w': nrm(ks[25], (DEPTH, CF_WIDTH, C), CF_WIDTH ** -0.5),
        'cf_ln_g': 1.0 + nrm(ks[26], (DEPTH, C), 0.01),
        'cf_ln_b': nrm(ks[27], (DEPTH, C), 0.01),
        'w_branch': nrm(ks[28], (DEPTH, N_BRANCH, C, D), C ** -0.5),
        'w_out': nrm(ks[29], (DEPTH, D, D), D ** -0.5),
        'w_ffn_in': nrm(ks[30], (DEPTH, D, 2 * FF_DIM), D ** -0.5),
        'w_ffn_out': nrm(ks[31], (DEPTH, FF_DIM, D), FF_DIM ** -0.5),
    }


def reference(x_prompt, x_sample, state_pool, state_ssm_re, state_ssm_im, state_shortconv, state_conformer,
              norm1_g, norm2_g, final_g, w_in, b_gate, pool_w, pool_scale, lam_re, lam_im, log_dt,
              b_re, b_im, c_re, c_im, d_skip, w_glu, b_glu, sc_w, cf_w, cf_ln_g, cf_ln_b,
              w_branch, w_out, w_ffn_in, w_ffn_out):
    weights = (norm1_g, norm2_g, final_g, w_in, b_gate, pool_w, pool_scale, lam_re, lam_im, log_dt,
               b_re, b_im, c_re, c_im, d_skip, w_glu, b_glu, sc_w, cf_w, cf_ln_g, cf_ln_b,
               w_branch, w_out, w_ffn_in, w_ffn_out)
    nb = x_prompt.shape[0]
    sd = state_pool.dtype
    z_pool = jnp.zeros((DEPTH, nb, POOL_BUF, MIX_W), sd)
    z_ssm = jnp.zeros((DEPTH, nb, SSM_GROUPS, SSM_STATE), state_ssm_re.dtype)
    z_sc = jnp.zeros((DEPTH, nb, SC_WIDTH - 1, MIX_W), state_shortconv.dtype)
    z_cf = jnp.zeros((DEPTH, nb, CF_WIDTH - 1, MIX_W), state_conformer.dtype)
    y_prompt, pool_p, ssm_re_p, ssm_im_p, sc_p, cf_p = trunk(x_prompt, 0, z_pool, z_ssm, z_ssm, z_sc, z_cf, weights)
    y_sample, pool_s, ssm_re_s, ssm_im_s, sc_s, cf_s = trunk(x_sample, PAST_LEN, state_pool, state_ssm_re, state_ssm_im,
                                                             state_shortconv, state_conformer, weights)
    return (y_prompt, y_sample, pool_p, pool_s, ssm_re_p, ssm_re_s, ssm_im_p, ssm_im_s, sc_p, sc_s, cf_p, cf_s)
```

```python
import numpy as np
import concourse.bass as bass
import concourse.mybir as mybir
from concourse.ap import AP
from concourse.bass_utils import run_bass_kernel_spmd

F32 = mybir.dt.float32
BF16 = mybir.dt.bfloat16
I32 = mybir.dt.int32
AF = mybir.ActivationFunctionType
ALU = mybir.AluOpType

NCORES = 8
D = 1024
DT = 8
DEPTH = 4
TP = 2048
NS = 16
LS = 8
T = TP + NS * LS
C = 256
IN_COLS = 5888
OFF_SSM, OFF_SC, OFF_CF, OFF_GATE = 256, 512, 1280, 1792
FF = 2816
NFT = 22
BLKS = [(0, 512), (512, 512), (1024, 512), (1536, 512), (2048, 128)]
NCH = T // 8
USE_ARS = True
TWO_PI = 6.283185307179586


class _Rec:
    def __init__(self):
        self.call = None

    def __getattr__(self, name):
        def f(*a, **k):
            self.call = (name, a, k)
            return self
        return f


def _freeze(fn):
    if fn is None:
        return None
    r = _Rec()
    fn(r)
    name, a, k = r.call
    return lambda e: getattr(e, name)(*a, **k)


class Coop:
    def __init__(self, fn):
        import threading
        self.go = threading.Semaphore(0)
        self.done = threading.Semaphore(0)
        self.finished = False
        self.exc = None

        def run():
            try:
                self.go.acquire()
                fn()
            except BaseException as ex:
                self.exc = ex
            self.finished = True
            self.done.release()
        self.t = threading.Thread(target=run, daemon=True)
        self.t.start()

    def inside(self):
        import threading
        return threading.current_thread() is self.t

    def tick(self):
        self.done.release()
        self.go.acquire()

    def step(self, n):
        for _ in range(n):
            if self.finished:
                break
            self.go.release()
            self.done.acquire()
        if self.exc is not None:
            raise self.exc

    def finish(self):
        while not self.finished:
            self.go.release()
            self.done.acquire()
        if self.exc is not None:
            raise self.exc


class Emit:
    COMPUTE = ("pe", "act", "dve", "pool")
    coop = None

    def __init__(self, nc, nchan=24):
        self.nc = nc
        self.eng = {"pe": nc.tensor, "act": nc.scalar, "dve": nc.vector, "pool": nc.gpsimd, "sp": nc.sync}
        self.lists = {e: [] for e in self.eng}
        self.ticks = {e: 0 for e in self.COMPUTE}
        self.pending = {e: False for e in self.COMPUTE}
        self.waited = {e: {} for e in self.eng}
        self.lastw = {}
        self.readers = {}
        self.chanv = {}
        self.nops = 0

    def _need(self, engname, ev, waits):
        if ev is None:
            return
        sem, val, src = ev
        if src == engname and engname == "pe":
            return
        if self.waited[engname].get(sem, 0) >= val:
            return
        waits.append((sem, val))

    def _deps(self, engname, reads, writes, same_eng_war=False):
        waits = []
        for r in reads:
            self._need(engname, self.lastw.get(r), waits)
        for w in writes:
            self._need(engname, self.lastw.get(w), waits)
            for ev in self.readers.get(w, ()):
                self._need(engname, ev, waits)
        best = {}
        for s, v in waits:
            best[s] = max(best.get(s, 0), v)
        for s, v in best.items():
            self.waited[engname][s] = max(self.waited[engname].get(s, 0), v)
        return list(best.items())

    def _commit(self, ev, reads, writes):
        for r in reads:
            self.readers.setdefault(r, []).append(ev)
        for w in writes:
            self.lastw[w] = ev
            self.readers[w] = []

    def op(self, engname, fn, r=(), w=(), sig=True):
        waits = self._deps(engname, r, w)
        if sig:
            self.ticks[engname] += 1
            ev = (engname, self.ticks[engname], engname)
        else:
            ev = (engname, self.ticks[engname] + 1, engname)
        self.pending[engname] = not sig
        self.lists[engname].append((_freeze(fn), waits, (engname, 1) if sig else None))
        self._commit(ev, r, w)
        self.nops += 1
        if self.coop is not None and self.coop.inside():
            self.coop.tick()

    def dma(self, queue, fn, r=(), w=(), chan="c0"):
        waits = self._deps(queue, r, w, same_eng_war=True)
        prev = self.chanv.get(chan, 0)
        if prev > self.waited[queue].get(chan, 0):
            waits = [(s_, v_) for s_, v_ in waits if s_ != chan] + [(chan, prev)]
            self.waited[queue][chan] = prev
        self.chanv[chan] = self.chanv.get(chan, 0) + 16
        ev = (chan, self.chanv[chan], "dma")
        self.lists[queue].append((_freeze(fn), waits, (chan, 16)))
        self._commit(ev, r, w)
        self.nops += 1
        if self.coop is not None and self.coop.inside():
            self.coop.tick()

    def barrier(self):
        for e in self.COMPUTE:
            assert not self.pending[e], f"engine {e} has unsignalled tail op at barrier"
        for e in self.eng:
            if e == "pe":
                continue
            waits = []
            for o in self.COMPUTE:
                if o != e and self.ticks[o] > self.waited[e].get(o, 0):
                    waits.append((o, self.ticks[o]))
                    self.waited[e][o] = self.ticks[o]
            for c, v in self.chanv.items():
                if v > self.waited[e].get(c, 0):
                    waits.append((c, v))
                    self.waited[e][c] = v
            if waits:
                self.lists[e].append((None, waits, None))

    def final_wait(self, engname="sp"):
        waits = [(c, v) for c, v in self.chanv.items()]
        self.lists[engname].append((None, waits, None))

    def emit(self):
        nc = self.nc
        from contextlib import ExitStack
        with ExitStack() as es:
            sems = {}
            for name in list(self.COMPUTE) + sorted(self.chanv.keys()):
                sems[name] = es.enter_context(nc.semaphore("s_" + name))
            block = es.enter_context(nc.Block())

            def run(engname):
                def body(e):
                    for fn, waits, inc in self.lists[engname]:
                        for s, v in waits:
                            e.wait_ge(sems[s], v)
                        if fn is not None:
                            ins = fn(e)
                            if inc is not None:
                                ins.then_inc(sems[inc[0]], inc[1])
                return body

            block.tensor(run("pe"))
            block.scalar(run("act"))
            block.vector(run("dve"))
            block.gpsimd(run("pool"))
            block.sync(run("sp"))


class Arena:
    def __init__(self, nc, words):
        self.t = nc.alloc_sbuf_tensor("arena", [128, words], F32)
        self.words = words
        self.top = 0
        self.peak = 0

    def mark(self):
        return self.top

    def release(self, m):
        self.top = m

    def alloc(self, shape, dtype):
        n = 1
        for s in shape:
            n *= s
        bpe = 4 if dtype in (F32, I32) else 2
        w = (n * bpe + 3) // 4
        w = (w + 7) // 8 * 8
        off = self.top
        self.top += w
        self.peak = max(self.peak, self.top)
        assert self.top <= self.words, f"arena overflow {self.top} > {self.words}"
        v = self.t.ap()[:, off:off + w]
        if dtype != F32:
            v = v.bitcast(dtype)
        v = v[:, 0:n]
        if len(shape) == 2:
            v = v.rearrange("p (a b) -> p a b", a=shape[0])
        elif len(shape) == 3:
            v = v.rearrange("p (a b c) -> p a b c", a=shape[0], b=shape[1])
        elif len(shape) == 4:
            v = v.rearrange("p (a b c d) -> p a b c d", a=shape[0], b=shape[1], c=shape[2])
        return v


def fv(ap, dims, off=0):
    return AP(ap.tensor, ap.offset + off, [list(ap.ap[0])] + [list(d) for d in dims])


def host_consts():
    c = {}
    c["ident"] = np.eye(128, dtype=np.float32)
    kv = np.array(list(range(7, -1, -1)) + list(range(-7, 1)) + list(range(1, 9)), dtype=np.float32)
    c["kv"] = np.tile(kv[None, :], (128, 1))
    j = np.arange(128) // 16
    c["maskc"] = (j[:, None] >= np.arange(8)[None, :]).astype(np.float32)
    pm = np.zeros((128, 2, 8, 128), np.float32)
    for k in range(128):
        e = (k // 16) % 2
        for g in range(8):
            pm[k, e, g, g * 16 + (k % 16)] = 1.0
    c["pmat"] = pm
    wins = (2, 4, 8, 16)
    coef = np.zeros((128, 2, 16), np.float32)
    r = np.ones((128, 2, 16), np.float32)
    for p in range(128):
        for t in range(2):
            win = wins[2 * t + (1 if p >= 64 else 0)]
            for i in range(16):
                if i < win:
                    coef[p, t, i] = 1.0 / win - (1.0 if i == 0 else 0.0)
                r[p, t, i] = win / min(i + 1, win)
    c["poolc"] = coef
    c["poolr"] = r
    c["poolrm1"] = r - 1.0
    return c


def build(nlayers=DEPTH, dbg=None):
    nc = bass.Bass("TRN2", target_bir_lowering=False)
    E = Emit(nc)

    def din(name, shape, dt=F32):
        return nc.dram_tensor(name, list(shape), dt, kind="ExternalInput").ap()

    def dout(name, shape, dt=F32):
        return nc.dram_tensor(name, list(shape), dt, kind="ExternalOutput").ap()

    L4 = DEPTH
    xin = din("xin", [T, D])
    st_pool = din("st_pool", [L4, NS * 15, C]); st_re = din("st_re", [L4, NS, 1024]); st_im = din("st_im", [L4, NS, 1024])
    st_sc = din("st_sc", [L4, NS * 2, C]); st_cf = din("st_cf", [L4, NS * 30, C])
    norm1_g = din("norm1_g", [L4, D]); norm2_g = din("norm2_g", [L4, D]); final_g = din("final_g", [D])
    w_in = din("w_in", [L4, D, IN_COLS]); b_gate = din("b_gate", [L4, 4, D])
    pool_w = din("pool_w", [L4, 4, 64, 64]); pool_scale = din("pool_scale", [L4, C])
    lam_re = din("lam_re", [L4, 16, 64]); lam_im = din("lam_im", [L4, 16, 64]); log_dt = din("log_dt", [L4, 16])
    b_re = din("b_re", [L4, 16, 64, 16]); b_im = din("b_im", [L4, 16, 64, 16])
    c_re = din("c_re", [L4, 16, 16, 64]); c_im = din("c_im", [L4, 16, 16, 64])
    d_skip = din("d_skip", [L4, C]); w_glu = din("w_glu", [L4, C, C]); b_glu = din("b_glu", [L4, C])
    sc_w = din("sc_w", [L4, 3, C]); cf_w = din("cf_w", [L4, 31, C]); cf_ln_g = din("cf_ln_g", [L4, C]); cf_ln_b = din("cf_ln_b", [L4, C])
    w_branch = din("w_branch", [L4, 4, C, D]); w_out = din("w_out", [L4, D, D])
    w_ffn_in = din("w_ffn_in", [L4, D, 2 * FF]); w_ffn_out = din("w_ffn_out", [L4, FF, D])
    ident_d = din("ident", [128, 128]); kv_d = din("kv", [128, 24]); maskc_d = din("maskc", [128, 8])
    pmat_d = din("pmat", [128, 2, 8, 128]); poolc_d = din("poolc", [128, 2, 16]); poolr_d = din("poolr", [128, 2, 16])
    poolrm1_d = din("poolrm1", [128, 2, 16])

    y_out = dout("y", [T, D])
    o_pool_p = dout("pool_p", [L4, 15, C]); o_pool_s = dout("pool_s", [L4, NS * 15, C])
    o_re_p = dout("re_p", [L4, 8, 128]); o_re_s = dout("re_s", [L4, NS, 1024])
    o_im_p = dout("im_p", [L4, 8, 128]); o_im_s = dout("im_s", [L4, NS, 1024])
    o_sc_p = dout("sc_p", [L4, 2, C]); o_sc_s = dout("sc_s", [L4, NS * 2, C])
    o_cf_p = dout("cf_p", [L4, 30, C]); o_cf_s = dout("cf_s", [L4, NS * 30, C])
    dbg_out = dout("dbg", [4, 2, 128, T]) if dbg else None
    ssmw_d = nc.dram_tensor("ssmw_scr", [L4, 128, 10240], BF16, kind="Internal").ap()

    A = Arena(nc, 52736)
    psum_t = nc.alloc_psum_tensor("ps", [128, 8, 512], F32)
    PS = psum_t.ap()
    PSB = PS.rearrange("p b n -> p (b n)").bitcast(BF16).rearrange("p (b n) -> p b n", b=8)
    bank_ctr = [0]

    held = set()

    def bank():
        if E.coop is not None:
            if E.coop.inside():
                return 7
        while True:
            b = bank_ctr[0] % 8
            bank_ctr[0] += 1
            if b in held or (E.coop is not None and b == 7):
                continue
            return b

    def P(b):
        return ("ps", b)

    xT = A.alloc([DT, T], F32)
    hB = A.alloc([DT, T], BF16)
    identF = A.alloc([128], F32)
    identB = A.alloc([128], BF16)
    onesB = A.alloc([128], BF16)
    cst = A.alloc([8], F32)
    gfin = A.alloc([DT], F32)
    parT = A.alloc([L4, 128], F32)
    scoef = A.alloc([L4, 8, 8, 3], F32)
    maskc = A.alloc([8], F32)
    poolc = A.alloc([2, 16], F32); poolr = A.alloc([2, 16], F32); poolrm1 = A.alloc([2, 16], F32)
    ws = A.alloc([3, 2560], BF16)
    ws_ctr = [0]

    ld_rr = [0]

    def ld(out, in_, w, chan=None, slow=False, q="sp", r=()):
        if chan is None:
            chan = "ld%d" % (ld_rr[0] % 8)
            ld_rr[0] += 1
        if slow:
            E.dma(q, lambda e: e.dma_start(out=out, in_=in_, allow_slow_non_contiguous=True), w=w, r=r, chan=chan)
        else:
            E.dma(q, lambda e: e.dma_start(out=out, in_=in_), w=w, r=r, chan=chan)

    ld(identF, ident_d, ["identF"])
    ld(identB, ident_d, ["identB"], chan="ldp", q="pool")
    ld(maskc, maskc_d, ["maskc"]); ld(poolc, poolc_d, ["poolc"]); ld(poolr, poolr_d, ["poolc"]); ld(poolrm1, poolrm1_d, ["poolc"])
    E.op("dve", lambda e: e.memset(onesB, 1.0), w=["onesB"])
    E.op("dve", lambda e: e.memset(cst[:, 0:1], 1e-6), w=["cst"])
    E.op("dve", lambda e: e.memset(cst[:, 1:2], 1e-5), w=["cst"])
    E.op("dve", lambda e: e.memset(cst[:, 2:3], 0.0), w=["cst"])
    ld(gfin, final_g.rearrange("(t p) -> p t", p=128), ["gfin"], slow=True)

    def wslot():
        i = ws_ctr[0] % 3
        ws_ctr[0] += 1
        return i

    def wdma(slot, off, shape, src):
        n = 1
        for s in shape:
            n *= s
        v = ws[:, slot, off:off + n]
        if len(shape) == 2:
            v = v.rearrange("p (a b) -> p a b", a=shape[0])
        elif len(shape) == 3:
            v = v.rearrange("p (a b c) -> p a b c", a=shape[0], b=shape[1])
        E.dma("pool", lambda e: e.dma_start(out=v, in_=src), w=[("ws", slot)], chan="w%d" % slot)
        return v

    m0 = A.mark()
    stg = A.alloc([2, D], F32)
    for i in range(T // 128):
        sb = i % 2
        E.dma("sp", lambda e, i=i, sb=sb: e.dma_start(out=stg[:, sb, :], in_=xin[i * 128:(i + 1) * 128, :]),
              w=[("stg", sb)], chan="ldx%d" % sb)
        for half in range(2):
            b = bank()
            for q in range(4):
                dt_ = half * 4 + q
                E.op("pe", lambda e, b=b, q=q, dt_=dt_, sb=sb: e.transpose(
                    out=PS[:, b, q * 128:(q + 1) * 128], in_=stg[:, sb, dt_ * 128:(dt_ + 1) * 128],
                    identity=identF), r=[("stg", sb), "identF"], w=[P(b)], sig=(q == 3))
            if half:
                E.op("act", lambda e, b=b, half=half, i=i: e.activation(
                    out=xT[:, half * 4:half * 4 + 4, i * 128:(i + 1) * 128],
                    in_=PS[:, b, :].rearrange("p (a c) -> p a c", a=4), func=AF.Copy), r=[P(b)], w=[("xT", i // 4 if i < 16 else 4)])
            else:
                E.op("dve", lambda e, b=b, half=half, i=i: e.tensor_copy(
                    out=xT[:, half * 4:half * 4 + 4, i * 128:(i + 1) * 128],
                    in_=PS[:, b, :].rearrange("p (a c) -> p a c", a=4)), r=[P(b)], w=[("xT", i // 4 if i < 16 else 4)])
    for l in range(nlayers):
        prow = stg[:, l % 2, 0:128]
        srcs = [(0, norm1_g[l].rearrange("(r p) -> r p", p=128)), (8, norm2_g[l].rearrange("(r p) -> r p", p=128)),
                (16, b_gate[l].rearrange("k (r p) -> (k r) p", p=128)), (48, pool_scale[l].rearrange("(r p) -> r p", p=128)),
                (50, b_glu[l].rearrange("(r p) -> r p", p=128)), (52, cf_ln_g[l].rearrange("(r p) -> r p", p=128)),
                (54, cf_ln_b[l].rearrange("(r p) -> r p", p=128)), (56, d_skip[l].rearrange("(r p) -> r p", p=128)),
                (58, sc_w[l].rearrange("k (r p) -> (k r) p", p=128)), (64, cf_w[l].rearrange("k (r p) -> (k r) p", p=128))]
        keys = [("prow", l % 2, r0) for r0, _ in srcs]
        E.op("dve", lambda e, prow=prow: e.memset(prow, 0.0), r=[], w=[("stg", l % 2)] + keys)
        for (r0, s_), k_ in zip(srcs, keys):
            nr = s_.shape[0]
            ld(prow[r0:r0 + nr, :], s_, [k_], chan="ldx%d" % (l % 2))
        b = bank()
        E.op("pe", lambda e, b=b, prow=prow: e.transpose(out=PS[:, b, 0:128], in_=prow, identity=identF),
             r=keys + ["identF"], w=[P(b)])
        E.op("dve", lambda e, b=b, l=l: e.tensor_copy(out=parT[:, l, :], in_=PS[:, b, 0:128]), r=[P(b)], w=["parT"])
        E.op("dve", lambda e: e.memset(cst[:, 3:4], 0.0), r=keys, w=[("stg", l % 2)])
    E.barrier()
    A.release(m0)

    def par(l, r0, n=1):
        return parT[:, l, r0:r0 + n]
    kvt = A.alloc([24], F32)
    ld(kvt, kv_d, ["kvt"])

    def prep_layer(l):
        WbT = A.alloc([2, 8, 8, 32], BF16)
        WaTs = A.alloc([2, 2, 8, 128], BF16)
        WbT2 = A.alloc([2, 8, 8, 32], BF16)
        E.op("dve", lambda e: e.memset(WbT2.rearrange("p a b c d -> p (a b c d)"), 0.0), w=["WbT"])
        E.op("dve", lambda e: e.memset(WbT.rearrange("p a b c d -> p (a b c d)"), 0.0), w=["WbT"])
        E.op("dve", lambda e: e.memset(WaTs.rearrange("p a b c d -> p (a b c d)"), 0.0), w=["WaTs"])
        lamr = A.alloc([8], F32); lami = A.alloc([8], F32); ldt = A.alloc([8], F32)
        bre = A.alloc([8, 16], F32); bim = A.alloc([8, 16], F32); cre = A.alloc([8, 16], F32); cim = A.alloc([8, 16], F32)
        dcol = A.alloc([16], F32)
        ld(lamr, AP(lam_re.tensor, l * 1024, [[1, 128], [128, 8]]), [("lamr", l)], slow=True)
        ld(lami, AP(lam_im.tensor, l * 1024, [[1, 128], [128, 8]]), [("lami", l)], slow=True)
        for e_ in range(2):
            ld(ldt[64 * e_:64 * e_ + 64, :], AP(log_dt.tensor, l * 16 + e_, [[0, 64], [2, 8]]), [("ldt", l, e_)], slow=True)
            for q_ in range(8):
                ld(cre[64 * e_:64 * e_ + 64, q_, :], AP(c_re.tensor, l * 16384 + e_ * 1024 + q_ * 2048, [[1, 64], [64, 16]]), [("cre", l, e_, q_)], slow=True)
                ld(cim[64 * e_:64 * e_ + 64, q_, :], AP(c_im.tensor, l * 16384 + e_ * 1024 + q_ * 2048, [[1, 64], [64, 16]]), [("cim", l, e_, q_)], slow=True)
        ld(bre, AP(b_re.tensor, l * 16384, [[16, 128], [2048, 8], [1, 16]]), [("bre", l)], slow=True)
        ld(bim, AP(b_im.tensor, l * 16384, [[16, 128], [2048, 8], [1, 16]]), [("bim", l)], slow=True)
        for j_ in range(8):
            ld(dcol[16 * j_:16 * j_ + 16, :], AP(d_skip.tensor, l * 256, [[1, 16], [16, 16]]), [("dcol", l, j_)], slow=True)
        dck = [("dcol", l, j_) for j_ in range(8)]

        dtv = A.alloc([8], F32); ar = A.alloc([8], F32); ai = A.alloc([8], F32)
        t24a = A.alloc([8, 24], F32); t24b = A.alloc([8, 24], F32); t24i = A.alloc([8, 24], I32)
        Er = A.alloc([8, 24], F32); Ei = A.alloc([8, 24], F32); mag = A.alloc([8, 24], F32)
        E.op("act", lambda e: e.activation(out=dtv, in_=ldt, func=AF.Exp), r=[("ldt", l, 0), ("ldt", l, 1)], w=["dtv"])
        E.op("dve", lambda e: e.tensor_tensor(out=ar, in0=lamr, in1=dtv, op=ALU.mult), r=[("lamr", l), "dtv"], w=["ar"])
        E.op("dve", lambda e: e.tensor_tensor(out=ai, in0=lami, in1=dtv, op=ALU.mult), r=[("lami", l), "dtv"], w=["ai"])
        kvb = fv(kvt, [[0, 8], [1, 24]])
        E.op("dve", lambda e: e.tensor_tensor(out=t24a, in0=fv(ar, [[1, 8], [0, 24]]), in1=kvb, op=ALU.mult), r=["ar", "kvt"], w=["t24a"])
        E.op("act", lambda e: e.activation(out=mag, in_=t24a, func=AF.Exp), r=["t24a"], w=["mag"])
        E.op("dve", lambda e: e.tensor_tensor(out=t24a, in0=fv(ai, [[1, 8], [0, 24]]), in1=kvb, op=ALU.mult), r=["ai", "kvt", "mag"], w=["t24a"])
        for which, dst in ((0, Ei), (1, Er)):
            E.op("dve", lambda e, which=which: e.tensor_scalar(out=t24b, in0=t24a, scalar1=1.0 / TWO_PI, scalar2=0.25 * which,
                                                                 op0=ALU.mult, op1=ALU.add), r=["t24a"], w=["t24b"])
            E.op("dve", lambda e: e.tensor_copy(out=t24i, in_=t24b), r=["t24b"], w=["t24i"])
            E.op("dve", lambda e, dst=dst: e.tensor_copy(out=dst, in_=t24i), r=["t24i"], w=[("E", which)])
            E.op("dve", lambda e, dst=dst: e.tensor_tensor(out=t24b, in0=t24b, in1=dst, op=ALU.subtract), r=["t24b", ("E", which)], w=["t24b"])
            E.op("act", lambda e, dst=dst: e.activation(out=dst, in_=t24b, func=AF.Sin, scale=TWO_PI * (1.0 - 2e-6)), r=["t24b"], w=[("E", which)])
            E.op("dve", lambda e, dst=dst: e.tensor_tensor(out=dst, in0=dst, in1=mag, op=ALU.mult), r=[("E", which), "mag"], w=[("E", which)])
        EK = [("E", 0), ("E", 1)]
        sm = A.alloc([8, 8], F32)
        def S_(i):
            return fv(sm, [[8, 8]], off=i)
        a_r = fv(Er, [[24, 8]], off=16); a_i = fv(Ei, [[24, 8]], off=16)
        seq = [
            lambda e: e.tensor_scalar(out=S_(0), in0=a_r, scalar1=-1.0, scalar2=None, op0=ALU.add),
            lambda e: e.tensor_tensor(out=S_(1), in0=lamr, in1=lamr, op=ALU.mult),
            lambda e: e.tensor_tensor(out=S_(2), in0=lami, in1=lami, op=ALU.mult),
            lambda e: e.tensor_tensor(out=S_(1), in0=S_(1), in1=S_(2), op=ALU.add),
            lambda e: e.reciprocal(out=S_(1), in_=S_(1)),
            lambda e: e.tensor_tensor(out=S_(2), in0=S_(0), in1=lamr, op=ALU.mult),
            lambda e: e.tensor_tensor(out=S_(3), in0=a_i, in1=lami, op=ALU.mult),
            lambda e: e.tensor_tensor(out=S_(2), in0=S_(2), in1=S_(3), op=ALU.add),
            lambda e: e.tensor_tensor(out=S_(4), in0=S_(2), in1=S_(1), op=ALU.mult),
            lambda e: e.tensor_tensor(out=S_(2), in0=a_i, in1=lamr, op=ALU.mult),
            lambda e: e.tensor_tensor(out=S_(3), in0=S_(0), in1=lami, op=ALU.mult),
            lambda e: e.tensor_tensor(out=S_(2), in0=S_(2), in1=S_(3), op=ALU.subtract),
            lambda e: e.tensor_tensor(out=S_(5), in0=S_(2), in1=S_(1), op=ALU.mult),
        ]
        for fn in seq:
            E.op("dve", fn, r=EK + [("lamr", l), ("lami", l), "sm"], w=["sm"])
        Bbr = A.alloc([8, 16], F32); Bbi = A.alloc([8, 16], F32); tq = A.alloc([8, 16], F32)
        krb = fv(sm, [[8, 8], [0, 16]], off=4); kib = fv(sm, [[8, 8], [0, 16]], off=5)
        seq = [
            (lambda e: e.tensor_tensor(out=Bbr, in0=krb, in1=bre, op=ALU.mult), ["Bbr"]),
            (lambda e: e.tensor_tensor(out=tq, in0=kib, in1=bim, op=ALU.mult), ["tq"]),
            (lambda e: e.tensor_tensor(out=Bbr, in0=Bbr, in1=tq, op=ALU.subtract), ["Bbr"]),
            (lambda e: e.tensor_tensor(out=Bbi, in0=krb, in1=bim, op=ALU.mult), ["Bbi"]),
            (lambda e: e.tensor_tensor(out=tq, in0=kib, in1=bre, op=ALU.mult), ["tq"]),
            (lambda e: e.tensor_tensor(out=Bbi, in0=Bbi, in1=tq, op=ALU.add), ["Bbi"]),
        ]
        for fn, w_ in seq:
            E.op("dve", fn, r=["sm", ("bre", l), ("bim", l), "Bbr", "Bbi", "tq"], w=w_)
        P1 = A.alloc([8, 8, 16], F32); P2 = A.alloc([8, 8, 16], F32)

        def tab(Ex, base):
            return fv(Ex, [[24, 8], [1, 8], [0, 16]], off=base)

        def vec(Vx):
            return fv(Vx, [[16, 8], [0, 8], [1, 16]])

        def cplx(base, Vr, Vi, vkeys, out_re, out_im, neg_im, okeys):
            E.op("dve", lambda e: e.tensor_tensor(out=P1, in0=tab(Er, base), in1=vec(Vr), op=ALU.mult), r=EK + vkeys + ["P2"], w=["P1"])
            E.op("dve", lambda e: e.tensor_tensor(out=P2, in0=tab(Ei, base), in1=vec(Vi), op=ALU.mult), r=EK + vkeys + ["P1"], w=["P2"])
            out_re(ALU.subtract, okeys)
            E.op("dve", lambda e: e.tensor_tensor(out=P1, in0=tab(Er, base), in1=vec(Vi), op=ALU.mult), r=EK + vkeys + ["P2"] + okeys, w=["P1"])
            E.op("dve", lambda e: e.tensor_tensor(out=P2, in0=tab(Ei, base), in1=vec(Vr), op=ALU.mult), r=EK + vkeys + ["P1"] + okeys, w=["P2"])
            out_im(neg_im, okeys)

        def wbt_out(ri):
            def f(op_or_neg, okeys):
                op = op_or_neg if ri == 0 else ALU.add
                for e_ in range(2):
                    E.op("dve", lambda e, e_=e_: e.tensor_tensor(
                        out=WbT[64 * e_:64 * e_ + 64, ri, :, :, 16 * e_:16 * e_ + 16],
                        in0=P1[64 * e_:64 * e_ + 64], in1=P2[64 * e_:64 * e_ + 64], op=op), r=["P1", "P2"], w=["WbT", ("WbTl", l)])
                    E.op("dve", lambda e, e_=e_: e.tensor_tensor(
                        out=WbT2[64 * e_:64 * e_ + 64, ri, :, :, 16 * e_:16 * e_ + 16].rearrange("p s q c -> p q s c"),
                        in0=P1[64 * e_:64 * e_ + 64], in1=P2[64 * e_:64 * e_ + 64], op=op), r=["P1", "P2"], w=["WbT", ("WbTl", l)])
            return f
        cplx(0, Bbr, Bbi, ["Bbr", "Bbi"], wbt_out(0), wbt_out(1), False, ["WbT"])

        Gr = A.alloc([8, 128], BF16); nGi = A.alloc([8, 128], BF16)

        def gq_out(dst, is_im, key):
            dv = dst.rearrange("p q (j c) -> p q j c", j=8)
            def f(op_or_neg, okeys):
                if not is_im:
                    E.op("dve", lambda e: e.tensor_tensor(out=dv, in0=P1, in1=P2, op=ALU.subtract), r=["P1", "P2"], w=[key])
                else:
                    E.op("dve", lambda e: e.scalar_tensor_tensor(out=dv, in0=P1, scalar=-1.0, in1=P2, op0=ALU.mult, op1=ALU.subtract),
                         r=["P1", "P2"], w=[key])
            return f
        ck = [(n_, l, e_, q_) for n_ in ("cre", "cim") for e_ in range(2) for q_ in range(8)]
        cplx(8, cre, cim, ck, gq_out(Gr, False, "Gr"), gq_out(nGi, True, "nGi"), True, ["Gr", "nGi"])
        sc_l = scoef[:, l]
        def cf_(lev, c):
            return fv(sc_l, [[24, 8]], off=lev * 3 + c)
        E.op("dve", lambda e: e.tensor_copy(out=cf_(0, 0), in_=fv(Er, [[24, 8]], off=23)), r=EK, w=["scoef"])
        E.op("dve", lambda e: e.tensor_copy(out=cf_(0, 1), in_=fv(Ei, [[24, 8]], off=23)), r=EK, w=["scoef"])
        for lev in range(7):
            sq_ = [
                lambda e, lev=lev: e.tensor_tensor(out=S_(6), in0=cf_(lev, 0), in1=cf_(lev, 0), op=ALU.mult),
                lambda e, lev=lev: e.tensor_tensor(out=S_(7), in0=cf_(lev, 1), in1=cf_(lev, 1), op=ALU.mult),
                lambda e, lev=lev: e.tensor_tensor(out=cf_(lev + 1, 0), in0=S_(6), in1=S_(7), op=ALU.subtract),
                lambda e, lev=lev: e.tensor_tensor(out=S_(6), in0=cf_(lev, 0), in1=cf_(lev, 1), op=ALU.mult),
                lambda e, lev=lev: e.tensor_scalar(out=cf_(lev + 1, 1), in0=S_(6), scalar1=2.0, scalar2=None, op0=ALU.mult),
            ]
            for fn in sq_:
                E.op("dve", fn, r=["scoef", "sm"], w=["scoef", "sm"])
        for lev in range(8):
            E.op("dve", lambda e, lev=lev: e.tensor_scalar(out=cf_(lev, 2), in0=cf_(lev, 1), scalar1=-1.0, scalar2=None, op0=ALU.mult),
                 r=["scoef"], w=["scoef"])
        tmk = A.alloc([8, 16], F32)
        idv = fv(identF, [[16, 8], [1, 16]])
        for q in range(8):
            tile_, q4 = q // 4, q % 4
            b = bank()
            E.op("pe", lambda e, q=q, b=b: e.matmul(PS[:, b, 0:256], lhsT=Gr[:, q, :], rhs=WbT[:, 0, q].rearrange("p s c -> p (s c)"),
                                                      start=True, stop=False), r=["Gr", ("WbTl", l)], w=[P(b)], sig=False)
            E.op("pe", lambda e, q=q, b=b: e.matmul(PS[:, b, 0:256], lhsT=nGi[:, q, :], rhs=WbT[:, 1, q].rearrange("p s c -> p (s c)"),
                                                      start=False, stop=True), r=["nGi", ("WbTl", l)], w=[P(b)])
            for es in range(2):
                g = 2 * q + es
                psv = fv(PS[:, b, 0:256], [[32, 8], [1, 16]], off=16 * es)
                E.op("dve", lambda e, psv=psv: e.tensor_tensor(out=tmk, in0=psv, in1=fv(maskc, [[1, 8], [0, 16]]), op=ALU.mult),
                     r=[P(b), "maskc"], w=["tmk"])
                outv = fv(WaTs[:, tile_, es], [[128, 8], [1, 16]], off=q4 * 32 + 16 * es)
                E.op("dve", lambda e, outv=outv, g=g: e.scalar_tensor_tensor(out=outv, in0=idv, scalar=dcol[:, g:g + 1], in1=tmk,
                                                                               op0=ALU.mult, op1=ALU.add),
                     r=["tmk", "identF"] + dck, w=["WaTs", ("WaTsl", l)])
        wst = A.alloc([2, 1024], BF16)
        wst_i = [0]

        def flush(off, fn_fill, rkeys):
            i_ = wst_i[0] % 2
            wst_i[0] += 1
            fn_fill(wst[:, i_, :], ("wst", i_), rkeys)
            E.dma("sp", lambda e: e.dma_start(out=ssmw_d[l][:, off:off + 1024], in_=wst[:, i_, :]), r=[("wst", i_)], w=[("ssmw_d", l)], chan="ldw")
        for which in range(2):
            for tile_ in range(2):
                for x2 in range(2):
                    b = bank()
                    for s in range(8):
                        if which == 0:
                            src = WbT2[:, x2, s, tile_ * 4:tile_ * 4 + 4, :].rearrange("p q c -> p (q c)")
                            rk = [("WbTl", l)]
                        else:
                            src = WaTs[:, tile_, x2, s, :]
                            rk = [("WaTsl", l)]
                        E.op("pe", lambda e: e.transpose(out=PSB[:, b, s * 128:(s + 1) * 128], in_=src, identity=identB),
                             r=rk + ["identB"], w=[P(b)], sig=(s == 7))
                    off = which * 4096 + (tile_ * 2 + x2) * 1024
                    flush(off, lambda dst, k_, rk_: E.op("act", lambda e: e.activation(out=dst, in_=PSB[:, b, :], func=AF.Copy), r=rk_, w=[k_]), [P(b)])
        cplx(16, cre, cim, ck, gq_out(Gr, False, "Gr"), gq_out(nGi, True, "nGi"), True, ["Gr", "nGi"])
        flush(8192, lambda dst, k_, rk_: E.op("act", lambda e: e.activation(out=dst, in_=Gr.rearrange("p a b -> p (a b)"), func=AF.Copy), r=rk_, w=[k_]), ["Gr"])
        flush(9216, lambda dst, k_, rk_: E.op("act", lambda e: e.activation(out=dst, in_=nGi.rearrange("p a b -> p (a b)"), func=AF.Copy), r=rk_, w=[k_]), ["nGi"])

    m0 = A.mark()
    prep_layer(0)
    E.barrier()
    A.release(m0)
    E.barrier()
    def xkeys(bi):
        return [("xT", bi)]

    def rmsnorm(dst, dst_key, gcol_fn, gkey, scratch):
        sq2, rt2 = scratch
        for bi, (c0, n) in enumerate(BLKS):
            pb = bi % 2
            sq = sq2[:, pb]; rt = rt2[:, pb]
            for dt_ in range(DT):
                E.op("act", lambda e: e.activation(
                    out=sq[:, dt_, 0:n], in_=xT[:, dt_, c0:c0 + n], func=AF.Square), r=xkeys(bi), w=[("sq", pb, dt_)])
            b = bank()
            for dt_ in range(DT):
                E.op("pe", lambda e: e.matmul(
                    PS[:, b, 0:n], lhsT=onesB, rhs=sq[:, dt_, 0:n], start=(dt_ == 0), stop=(dt_ == DT - 1)),
                    r=[("sq", pb, dt_), "onesB"], w=[P(b)], sig=(dt_ == DT - 1))
            if USE_ARS:
                E.op("act", lambda e: e.activation(
                    out=rt[:, 0, 0:n], in_=PS[:, b, 0:n], func=AF.Ln, bias=cst[:, 0:1], scale=1.0 / D), r=[P(b), "cst"], w=[("rt0", pb)])
                E.op("act", lambda e: e.activation(
                    out=rt[:, 1, 0:n], in_=rt[:, 0, 0:n], func=AF.Exp, scale=-0.5), r=[("rt0", pb)], w=[("rt1", pb)])
            else:
                E.op("act", lambda e: e.activation(
                    out=rt[:, 0, 0:n], in_=PS[:, b, 0:n], func=AF.Sqrt, bias=cst[:, 0:1], scale=1.0 / D), r=[P(b), "cst"], w=[("rt0", pb)])
                E.op("dve", lambda e: e.reciprocal(out=rt[:, 1, 0:n], in_=rt[:, 0, 0:n]), r=[("rt0", pb)], w=[("rt1", pb)])
            for dt_ in range(DT):
                E.op("dve", lambda e: e.scalar_tensor_tensor(
                    out=dst[:, dt_, c0:c0 + n], in0=xT[:, dt_, c0:c0 + n], scalar=gcol_fn(dt_),
                    in1=rt[:, 1, 0:n], op0=ALU.mult, op1=ALU.mult), r=xkeys(bi) + [("rt1", pb), gkey], w=[(dst_key, bi)])

    def mm_group(bk, n, steps, rkeys, wkey=None):
        ns = len(steps)
        for i, st in enumerate(steps):
            lh, rh = st[0], st[1]
            tp = st[2] if len(st) > 2 else None
            if tp is None:
                fn = lambda e, lh=lh, rh=rh, i=i: e.matmul(PS[:, bk, 0:n], lhsT=lh, rhs=rh, start=(i == 0), stop=(i == ns - 1))
            else:
                fn = lambda e, lh=lh, rh=rh, i=i, tp=tp: e.matmul(PS[:, bk, 0:n], lhsT=lh, rhs=rh, start=(i == 0), stop=(i == ns - 1),
                                                                   tile_position=tp)
            E.op("pe", fn, r=rkeys, w=[P(bk)], sig=(i == ns - 1))

    pre = {}

    def prefetch_proj(l, col0):
        sl = wslot()
        wv = wdma(sl, 0, [8, 256], w_in[l][:, col0:col0 + 256].rearrange("(kt p) n -> p kt n", p=128))
        pre[("proj", l, col0)] = (sl, wv)

    def gate_w(l, j2, k):
        key = ("gate", l, j2, k)
        if key in pre:
            return pre.pop(key)
        sl = wslot()
        c0w = OFF_GATE + k * D + j2 * 256
        wg = wdma(sl, 0, [8, 256], w_in[l][:, c0w:c0w + 256].rearrange("(kt p) n -> p kt n", p=128))
        wb = wdma(sl, 2048, [2, 256], w_branch[l, k][:, j2 * 256:(j2 + 1) * 256].rearrange("(kt p) n -> p kt n", p=128))
        return (sl, wg, wb)

    def ffn_w(l, f):
        key = ("ffn", l, f)
        if key in pre:
            return pre.pop(key)
        sl = wslot()
        wg_ = wdma(sl, 0, [8, 128], w_ffn_in[l][:, f * 128:(f + 1) * 128].rearrange("(kt p) n -> p kt n", p=128))
        wu_ = wdma(sl, 1024, [8, 128], w_ffn_in[l][:, FF + f * 128:FF + (f + 1) * 128].rearrange("(kt p) n -> p kt n", p=128))
        return (sl, wg_, wu_)

    def proj_group(l, col0, evac):
        if ("proj", l, col0) not in pre:
            prefetch_proj(l, col0)
        sl, wv = pre.pop(("proj", l, col0))
        for ct in range(2):
            for bi, (c0, n) in enumerate(BLKS):
                bk = bank()
                mm_group(bk, n, [(wv[:, kt, ct * 128:(ct + 1) * 128], hB[:, kt, c0:c0 + n]) for kt in range(DT)],
                         [("ws", sl), ("h", bi)])
                evac(ct, bi, c0, n, bk)

    def ext_views(extp, exts, Hh, ct, bi, c0, n):
        if bi < 4:
            return extp[:, ct, Hh + c0:Hh + c0 + n]
        return exts[:, ct, Hh:Hh + LS, :].rearrange("p j b -> p b j")

    def load_hist(l, src_d, Hh, exts, key):
        rows = NS * Hh
        nb_t = max(1, 128 // Hh) if Hh > 8 else NS
        nb_t = min(nb_t, NS)
        while NS % nb_t:
            nb_t -= 1
        m = A.mark()
        hst = A.alloc([2, C], F32)
        for ti, b0 in enumerate(range(0, NS, nb_t)):
            nr = nb_t * Hh
            sb = ti % 2
            E.dma("sp", lambda e, b0=b0, nr=nr, sb=sb: e.dma_start(out=hst[0:nr, sb, :], in_=src_d[l, b0 * Hh:b0 * Hh + nr, :]),
                  w=[("hst", sb)], chan="lh%d" % sb)
            for ct in range(2):
                bk = bank()
                E.op("pe", lambda e, bk=bk, nr=nr, sb=sb, ct=ct: e.transpose(out=PS[:, bk, 0:nr], in_=hst[0:nr, sb, ct * 128:(ct + 1) * 128],
                                                                               identity=identF[0:nr, 0:nr]),
                     r=[("hst", sb), "identF"], w=[P(bk)])
                E.op("dve", lambda e, bk=bk, nr=nr, ct=ct, b0=b0: e.tensor_copy(
                    out=exts[:, ct, 0:Hh, b0:b0 + nb_t].rearrange("p r b -> p b r"), in_=PS[:, bk, 0:nr].rearrange("p (b r) -> p b r", r=Hh)),
                    r=[P(bk)], w=[key])

    def store_hist(l, Hh, extp, exts, key, out_p, out_s):
        m = A.mark()
        ost = A.alloc([8, C], F32)
        for ct in range(2):
            bk = bank()
            E.op("pe", lambda e: e.transpose(out=PSB[0:Hh, bk, 0:128], in_=extp[:, ct, TP:TP + Hh], identity=identB),
                 r=[key, "identB"], w=[P(bk)])
            E.op("act", lambda e: e.activation(out=ost[0:Hh, 0, ct * 128:(ct + 1) * 128], in_=PSB[0:Hh, bk, 0:128], func=AF.Copy),
                 r=[P(bk)], w=["ost"])
        E.dma("sp", lambda e: e.dma_start(out=out_p[l], in_=ost[0:Hh, 0, :]), r=["ost"], chan="so0")
        for b0 in (0, 8):
            for ct in range(2):
                bk = bank()
                for bb in range(8):
                    E.op("pe", lambda e: e.transpose(out=PSB[0:Hh, bk, bb * 128:(bb + 1) * 128], in_=exts[:, ct, LS:LS + Hh, b0 + bb], identity=identB),
                         r=[key, "identB"], w=[P(bk)], sig=(bb == 7))
                E.op("act", lambda e: e.activation(out=ost[0:Hh, :, ct * 128:(ct + 1) * 128],
                                                   in_=PSB[0:Hh, bk, :].rearrange("p (b c) -> p b c", c=128), func=AF.Copy),
                     r=[P(bk)], w=["ost"])
            dst = out_s[l, b0 * Hh:(b0 + 8) * Hh, :].rearrange("(b r) c -> r b c", r=Hh)
            E.dma("sp", lambda e: e.dma_start(out=dst, in_=ost[0:Hh, :, :]), r=["ost"], chan="so0")

    def diag_build(dst, wcol_fn, ntap, key):
        for k in range(ntap):
            E.op("act", lambda e, k=k: e.activation(out=dst[:, k, :], in_=identF, func=AF.Copy, scale=wcol_fn(k)),
                 r=["identF", "parT"], w=[key])

    def conv_taps(ntap, Hh, dg, extp, exts, ct, bi, c0, n):
        steps = []
        for k in range(ntap):
            sh = Hh - (ntap - 1) + k
            if bi < 4:
                rh = extp[:, ct, sh + c0:sh + c0 + n]
            else:
                rh = exts[:, ct, sh:sh + LS, :].rearrange("p j b -> p (j b)")
            steps.append((dg[:, ct, k, :], rh))
        return steps
    GELU_C = 1.5957691216057308
    for l in range(nlayers):
        m_l = A.mark()
        ybr = [None] * 4
        ybr[1] = A.alloc([2, T], BF16)
        m_s = A.mark()
        Wall = A.alloc([10240], BF16)
        E.dma("sp", lambda e, l=l: e.dma_start(out=Wall, in_=ssmw_d[l]), r=[("ssmw_d", l)], w=["Wall"], chan="ldw")
        m_n = A.mark()
        sq = A.alloc([2, DT, 512], BF16); rt = A.alloc([2, 2, 512], F32)
        rmsnorm(hB, "h", lambda dt_: par(l, dt_), "parT", (sq, rt))
        A.release(m_n)
        prefetch_proj(l, OFF_SSM)
        E.barrier()
        Wb_l = Wall[:, 0:4096].rearrange("p (t r s c) -> p t r s c", t=2, r=2, s=8)
        Wa_l = Wall[:, 4096:8192].rearrange("p (t r s c) -> p t r s c", t=2, r=2, s=8)
        Qr = Wall[:, 8192:9216].rearrange("p (q c) -> p q c", q=8)
        nQi = Wall[:, 9216:10240].rearrange("p (q c) -> p q c", q=8)
        u_fm = A.alloc([2, 8, NCH], BF16)
        z_fm = A.alloc([2, T], BF16)
        h0 = A.alloc([2, 8, NS], F32)
        fin_s = A.alloc([2, 8, NS], F32)
        fin_p = A.alloc([2, 8], F32)
        Ych = A.alloc([8, NCH], BF16)
        hs_st1 = A.alloc([1024], F32)
        hs_st = fv(hs_st1, [[0, 2], [1, 1024]])
        pmat = A.alloc([2, 8, 128], BF16)
        ld(pmat, pmat_d, ["pmat"], chan="ldp", q="pool")
        for ri, src in enumerate((st_re, st_im)):
            E.dma("sp", lambda e, ri=ri, src=src: e.dma_start(out=hs_st[0:NS, ri, :], in_=src[l]), w=["hs_st"], chan="lh%d" % ri)
            bk = bank()
            for q in range(8):
                E.op("pe", lambda e, bk=bk, q=q, ri=ri: e.transpose(out=PS[:, bk, q * NS:(q + 1) * NS], in_=hs_st[0:NS, ri, q * 128:(q + 1) * 128],
                                                                     identity=identF[0:NS, 0:NS]), r=["hs_st", "identF"], w=[P(bk)], sig=(q == 7))
            E.op("dve", lambda e, bk=bk, ri=ri: e.tensor_copy(out=h0[:, ri].rearrange("p q b -> p (q b)"), in_=PS[:, bk, 0:128]),
                 r=[P(bk)], w=["h0"])

        def ev_u(ct, bi, c0, n, bk):
            E.op("act", lambda e: e.activation(out=u_fm[:, ct, :, c0 // 8:(c0 + n) // 8].rearrange("p s c -> p c s"),
                                               in_=PS[:, bk, 0:n].rearrange("p (c s) -> p c s", s=8), func=AF.Copy), r=[P(bk)], w=[("u_fm", ct)])
        proj_group(l, OFF_SSM, ev_u)

        def usub(tile_, q4, s):
            return u_fm[32 * q4:32 * q4 + 32, tile_, s, :]

        sbuf_ = A.alloc([2, 2, 2, 384], F32)
        E.op("dve", lambda e: e.memset(sbuf_.rearrange("p a b c d -> p (a b c d)"), 0.0), w=[("sb", a_, b_, c_) for a_ in range(2) for b_ in range(2) for c_ in range(2)])
        Ssm = A.alloc([2, 2, NS], F32)
        HpB = A.alloc([2, 2, NCH], BF16)
        Y2s = A.alloc([2, NCH], F32)
        pendY2 = []

        def emit_Y2(q, qp):
            tile_, q4 = q // 4, q % 4
            for e_ in range(2):
                g = 2 * q + e_
                bk1 = Y1banks[(q, e_)]
                bk2 = bank()
                mm_group(bk2, NCH, [(Qr[64 * e_:64 * e_ + 64, q, :], HpB[64 * e_:64 * e_ + 64, qp, 0, :], (64 * e_, 0)),
                                    (nQi[64 * e_:64 * e_ + 64, q, :], HpB[64 * e_:64 * e_ + 64, qp, 1, :], (64 * e_, 0))],
                         ["Wall", ("HpB", qp)])
                E.op("act", lambda e, bk2=bk2, e_=e_: e.activation(out=Y2s[:, e_, :], in_=PS[:, bk2, 0:NCH], func=AF.Copy),
                     r=[P(bk2)], w=[("Y2s", e_)])
                E.op("dve", lambda e, bk1=bk1, e_=e_, g=g: e.tensor_tensor(out=Ych[:, g % 8, :], in0=PS[:, bk1, 0:NCH], in1=Y2s[:, e_, :], op=ALU.add),
                     r=[P(bk1), ("Y2s", e_)], w=[("Ych", g % 8)])
                held.discard(bk1)

        def placement(tile_):
            for j in range(8):
                bk = bank()
                rb = 32 * (j // 2)
                mm_group(bk, NCH, [(pmat[rb:rb + 32, j % 2, g8, :], Ych[rb:rb + 32, g8, :], (rb, 0)) for g8 in range(8)],
                         ["pmat"] + [("Ych", g8) for g8 in range(8)])
                E.op("act", lambda e, bk=bk: e.activation(out=gl[:, 0, :], in_=PS[:, bk, 0:NCH], func=AF.Square, scale=0.044715 ** 0.5), r=[P(bk)], w=["gl0"])
                E.op("dve", lambda e, bk=bk: e.scalar_tensor_tensor(out=gl[:, 1, :], in0=gl[:, 0, :], scalar=1.0, in1=PS[:, bk, 0:NCH], op0=ALU.add, op1=ALU.mult),
                     r=["gl0", P(bk)], w=["gl1"])
                E.op("act", lambda e: e.activation(out=gl[:, 2, :], in_=gl[:, 1, :], func=AF.Sigmoid, scale=GELU_C), r=["gl1"], w=["gl2"])
                E.op("dve", lambda e, bk=bk, j=j: e.tensor_tensor(out=fv(z_fm[:, tile_, :], [[8, NCH]], off=j), in0=gl[:, 2, :], in1=PS[:, bk, 0:NCH], op=ALU.mult),
                     r=["gl2", P(bk)], w=[("z_fm", tile_)])

        gl = A.alloc([3, NCH], F32)
        Y1banks = {}
        for q in range(8):
            tile_, q4, qp = q // 4, q % 4, q % 2
            rows = slice(32 * q4, 32 * q4 + 32)
            tp = (32 * q4, 0)
            if q == 4:
                emit_Y2(*pendY2.pop())
                placement(0)
            for ri in range(2):
                bk = bank()
                mm_group(bk, NCH, [(Wb_l[rows, tile_, ri, s, :], usub(tile_, q4, s), tp) for s in range(8)], ["Wall", ("u_fm", tile_)])
                if ri == 0:
                    E.op("act", lambda e, bk=bk, qp=qp: e.activation(out=sbuf_[:, qp, 0, 0, 128:384], in_=PS[:, bk, 0:256], func=AF.Copy),
                         r=[P(bk)], w=[("sb", qp, 0, 0)])
                    E.op("act", lambda e, bk=bk, qp=qp: e.activation(out=Ssm[:, qp, 0, :], in_=PS[:, bk, 256:NCH], func=AF.Copy),
                         r=[P(bk)], w=[("Ssm", qp)])
                else:
                    E.op("act", lambda e, bk=bk, qp=qp: e.activation(out=sbuf_[:, qp, 0, 1, 128:384], in_=PS[:, bk, 0:256], func=AF.Copy),
                         r=[P(bk)], w=[("sb", qp, 0, 1)])
                    E.op("act", lambda e, bk=bk, qp=qp: e.activation(out=Ssm[:, qp, 1, :], in_=PS[:, bk, 256:NCH], func=AF.Copy),
                         r=[P(bk)], w=[("Ssm", qp)])
            for e_ in range(2):
                bk = bank()
                Y1banks[(q, e_)] = bk
                held.add(bk)
                mm_group(bk, NCH, [(Wa_l[rows, tile_, e_, s, :], usub(tile_, q4, s), tp) for s in range(8)], ["Wall", ("u_fm", tile_)])
            if pendY2:
                emit_Y2(*pendY2.pop())
            cur = 0
            for lev in range(8):
                d = 1 << lev
                nxt = 1 - cur
                def cc(c):
                    return fv(scoef[:, l], [[1, 1]], off=q * 24 + lev * 3 + c)
                E.op("dve", lambda e: e.scalar_tensor_tensor(
                    out=sbuf_[:, qp, nxt, :, 128:384], in0=sbuf_[:, qp, cur, :, 128 - d:384 - d], scalar=cc(0), in1=sbuf_[:, qp, cur, :, 128:384],
                    op0=ALU.mult, op1=ALU.add), r=[("sb", qp, cur, 0), ("sb", qp, cur, 1), "scoef"], w=[("sb", qp, nxt, 0), ("sb", qp, nxt, 1)])
                for ri in range(2):
                    a_oth = sbuf_[:, qp, cur, 1 - ri, 128 - d:384 - d]
                    coth = cc(2) if ri == 0 else cc(1)
                    E.op("dve", lambda e: e.scalar_tensor_tensor(
                        out=sbuf_[:, qp, nxt, ri, 128:384], in0=a_oth, scalar=coth, in1=sbuf_[:, qp, nxt, ri, 128:384],
                        op0=ALU.mult, op1=ALU.add), r=[("sb", qp, cur, 0), ("sb", qp, cur, 1), ("sb", qp, nxt, ri), "scoef"], w=[("sb", qp, nxt, ri)])
                cur = nxt
            for ri in range(2):
                E.op("act", lambda e, ri=ri, qp=qp, cur=cur: e.activation(out=HpB[:, qp, ri, 0:256], in_=sbuf_[:, qp, cur, ri, 127:383], func=AF.Copy),
                     r=[("sb", qp, 0, ri)], w=[("HpB", qp)])
                E.op("act", lambda e, ri=ri, qp=qp, q=q: e.activation(out=HpB[:, qp, ri, 256:NCH], in_=h0[:, ri, q, :], func=AF.Copy),
                     r=["h0"], w=[("HpB", qp)])
                E.op("act", lambda e, ri=ri, qp=qp, q=q, cur=cur: e.activation(out=fin_p[:, ri, q:q + 1], in_=sbuf_[:, qp, cur, ri, 383:384], func=AF.Copy),
                     r=[("sb", qp, 0, ri)], w=["fin_p"])
            c0_ = fv(scoef[:, l], [[1, 1]], off=q * 24 + 0); c1_ = fv(scoef[:, l], [[1, 1]], off=q * 24 + 1); c2_ = fv(scoef[:, l], [[1, 1]], off=q * 24 + 2)
            for ri in range(2):
                E.op("dve", lambda e, ri=ri, qp=qp, q=q: e.scalar_tensor_tensor(out=fin_s[:, ri, q, :], in0=h0[:, ri, q, :], scalar=c0_, in1=Ssm[:, qp, ri, :],
                                                                                   op0=ALU.mult, op1=ALU.add), r=["h0", ("Ssm", qp), "scoef"], w=["fin_s"])
                E.op("dve", lambda e, ri=ri, qp=qp, q=q: e.scalar_tensor_tensor(out=fin_s[:, ri, q, :], in0=h0[:, 1 - ri, q, :], scalar=(c2_ if ri == 0 else c1_),
                                                                                   in1=fin_s[:, ri, q, :], op0=ALU.mult, op1=ALU.add),
                     r=["h0", "fin_s", "scoef"], w=["fin_s"])
            pendY2.append((q, qp))
        emit_Y2(*pendY2.pop())
        placement(1)
        for ri, (op_, os_) in enumerate(((o_re_p, o_re_s), (o_im_p, o_im_s))):
            bk = bank()
            E.op("pe", lambda e, bk=bk, ri=ri: e.transpose(out=PS[0:8, bk, 0:128], in_=fin_p[:, ri, :], identity=identF), r=["fin_p", "identF"], w=[P(bk)])
            E.op("act", lambda e, bk=bk, ri=ri: e.activation(out=hs_st[0:8, ri, 0:128], in_=PS[0:8, bk, 0:128], func=AF.Copy), r=[P(bk)], w=["hs_st"])
            E.dma("sp", lambda e, ri=ri, op_=op_: e.dma_start(out=op_[l], in_=hs_st[0:8, ri, 0:128]), r=["hs_st"], chan="so%d" % ri)
            bks = [bank(), bank()]
            for q in range(8):
                E.op("pe", lambda e, q=q, ri=ri, bks=bks: e.transpose(out=PS[0:NS, bks[q // 4], (q % 4) * 128:(q % 4 + 1) * 128], in_=fin_s[:, ri, q, :], identity=identF),
                     r=["fin_s", "identF"], w=[P(bks[q // 4])], sig=(q % 4 == 3))
            for hh in range(2):
                E.op("act", lambda e, ri=ri, hh=hh, bks=bks: e.activation(out=hs_st[0:NS, ri, hh * 512:(hh + 1) * 512], in_=PS[0:NS, bks[hh], :], func=AF.Copy),
                     r=[P(bks[hh])], w=["hs_st"])
            E.dma("sp", lambda e, ri=ri, os_=os_: e.dma_start(out=os_[l], in_=hs_st[0:NS, ri, :]), r=["hs_st"], chan="so%d" % ri)
        if dbg == "zfm" and l == 0:
            for ct in range(2):
                for wi, (src_, kk) in enumerate(((z_fm, "z_fm"),)):
                    for hf in range(4):
                        m_d = A.mark()
                        dtmp = A.alloc([544], F32)
                        E.op("dve", lambda e: e.tensor_copy(out=dtmp, in_=src_[:, ct, hf * 544:(hf + 1) * 544]), r=[(kk, ct)], w=["dtmp"])
                        E.dma("sp", lambda e: e.dma_start(out=dbg_out[wi, ct][:, hf * 544:(hf + 1) * 544], in_=dtmp), r=["dtmp"], chan="dbg")
                        E.barrier()
                        A.release(m_d)
        sl = wslot()
        wg = wdma(sl, 0, [2, 256], w_glu[l].rearrange("(kt p) n -> p kt n", p=128))
        sgt = fv(sbuf_, [[512, 2], [1, 512]])
        for ct in range(2):
            for bi, (c0, n) in enumerate(BLKS):
                bk = bank()
                mm_group(bk, n, [(wg[:, kt, ct * 128:(ct + 1) * 128], z_fm[:, kt, c0:c0 + n]) for kt in range(2)], [("ws", sl), ("z_fm", 0), ("z_fm", 1)])
                sb = bi % 2
                E.op("act", lambda e, bk=bk, n=n, sb=sb, ct=ct: e.activation(out=sgt[:, sb, 0:n], in_=PS[:, bk, 0:n], func=AF.Sigmoid, bias=par(l, 50 + ct)),
                     r=[P(bk), "parT"] + [("sb", 0, b_, c_) for b_ in range(2) for c_ in range(2)], w=[("sb", 0, b_, c_) for b_ in range(2) for c_ in range(2)] + [("sgt", sb)])
                E.op("dve", lambda e, n=n, sb=sb, ct=ct, c0=c0: e.tensor_tensor(out=ybr[1][:, ct, c0:c0 + n], in0=sgt[:, sb, 0:n], in1=z_fm[:, ct, c0:c0 + n], op=ALU.mult),
                     r=[("sgt", sb), ("z_fm", ct)], w=[("ybr", 1)])
        prefetch_proj(l, OFF_CF + 256)
        E.barrier()
        A.release(m_s)
        ybr[3] = A.alloc([2, T], BF16)
        m_c = A.mark()
        Hh = 30
        extp = A.alloc([2, Hh + TP], BF16); exts = A.alloc([2, Hh + LS, NS], BF16)
        sigb = A.alloc([2, T], BF16)
        dg = A.alloc([2, 31, 128], BF16)
        E.op("dve", lambda e: e.memset(extp[:, :, 0:Hh], 0.0), w=["cf_ext"])
        load_hist(l, st_cf, Hh, exts, "cf_ext")
        for ct in range(2):
            diag_build(dg[:, ct], lambda k, ct=ct: par(l, 64 + 2 * k + ct), 31, "cf_dg")

        def ev_b(ct, bi, c0, n, bk):
            E.op("act", lambda e: e.activation(out=sigb[:, ct, c0:c0 + n], in_=PS[:, bk, 0:n], func=AF.Sigmoid), r=[P(bk)], w=[("sigb", ct)])
        proj_group(l, OFF_CF + 256, ev_b)

        def ev_a(ct, bi, c0, n, bk):
            dst = ext_views(extp, exts, Hh, ct, bi, c0, n)
            i1 = sigb[:, ct, c0:c0 + n] if bi < 4 else sigb[:, ct, c0:c0 + n].rearrange("p (b j) -> p b j", j=LS)
            i0 = PS[:, bk, 0:n] if bi < 4 else PS[:, bk, 0:n].rearrange("p (b j) -> p b j", j=LS)
            E.op("dve", lambda e: e.tensor_tensor(out=dst, in0=i0, in1=i1, op=ALU.mult), r=[P(bk), ("sigb", ct)], w=["cf_ext"])
        proj_group(l, OFF_CF, ev_a)
        store_hist(l, Hh, extp, exts, "cf_ext", o_cf_p, o_cf_s)
        ycv = A.alloc([2, 2, 512], BF16)
        mu = A.alloc([512], F32); var = A.alloc([512], F32); t1 = A.alloc([512], F32)
        for bi, (c0, n) in enumerate(BLKS):
            for ct in range(2):
                bk = bank()
                mm_group(bk, n, conv_taps(31, Hh, dg, extp, exts, ct, bi, c0, n), ["cf_dg", "cf_ext"])
                E.op("act", lambda e: e.activation(out=ycv[:, ct, 0, 0:n], in_=PS[:, bk, 0:n], func=AF.Copy), r=[P(bk)], w=[("ycv", ct, 0)])
                E.op("act", lambda e: e.activation(out=ycv[:, ct, 1, 0:n], in_=PS[:, bk, 0:n], func=AF.Square), r=[P(bk)], w=[("ycv", ct, 1)])
            b1 = bank(); b2 = bank()
            mm_group(b1, n, [(onesB, ycv[:, ct, 0, 0:n]) for ct in range(2)], ["onesB", ("ycv", 0, 0), ("ycv", 1, 0)])
            mm_group(b2, n, [(onesB, ycv[:, ct, 1, 0:n]) for ct in range(2)], ["onesB", ("ycv", 0, 1), ("ycv", 1, 1)])
            E.op("act", lambda e: e.activation(out=mu[:, 0:n], in_=PS[:, b1, 0:n], func=AF.Copy, scale=1.0 / C), r=[P(b1)], w=["mu"])
            E.op("dve", lambda e: e.tensor_tensor(out=t1[:, 0:n], in0=mu[:, 0:n], in1=mu[:, 0:n], op=ALU.mult), r=["mu"], w=["t1"])
            E.op("dve", lambda e: e.scalar_tensor_tensor(out=var[:, 0:n], in0=PS[:, b2, 0:n], scalar=1.0 / C, in1=t1[:, 0:n], op0=ALU.mult, op1=ALU.subtract),
                 r=[P(b2), "t1"], w=["var"])
            E.op("act", lambda e: e.activation(out=var[:, 0:n], in_=var[:, 0:n], func=AF.Sqrt, bias=cst[:, 1:2]), r=["var", "cst"], w=["var"])
            E.op("dve", lambda e: e.reciprocal(out=var[:, 0:n], in_=var[:, 0:n]), r=["var"], w=["var"])
            for ct in range(2):
                E.op("dve", lambda e: e.tensor_tensor(out=t1[:, 0:n], in0=ycv[:, ct, 0, 0:n], in1=mu[:, 0:n], op=ALU.subtract), r=[("ycv", ct, 0), "mu"], w=["t1"])
                E.op("dve", lambda e: e.tensor_tensor(out=t1[:, 0:n], in0=t1[:, 0:n], in1=var[:, 0:n], op=ALU.mult), r=["t1", "var"], w=["t1"])
                o_ = ybr[3][:, ct, c0:c0 + n] if bi < 4 else ybr[3][:, ct, c0:c0 + n].rearrange("p (b j) -> p j b", j=LS)
                i_ = t1[:, 0:n] if bi < 4 else t1[:, 0:n].rearrange("p (j b) -> p j b", b=NS)
                E.op("act", lambda e: e.activation(out=o_, in_=i_, func=AF.Silu, bias=par(l, 54 + ct), scale=par(l, 52 + ct)),
                     r=["t1", "parT"], w=[("ybr", 3)])
        prefetch_proj(l, OFF_SC)
        E.barrier()
        A.release(m_c)
        ybr[2] = A.alloc([2, T], BF16)
        m_c = A.mark()
        Hh = 2
        extp = A.alloc([2, Hh + TP], BF16); exts = A.alloc([2, Hh + LS, NS], BF16)
        bgs = A.alloc([2, T], BF16); cgs = A.alloc([2, T], BF16)
        dg = A.alloc([2, 3, 128], BF16)
        E.op("dve", lambda e: e.memset(extp[:, :, 0:Hh], 0.0), w=["sc_ext"])
        load_hist(l, st_sc, Hh, exts, "sc_ext")
        for ct in range(2):
            diag_build(dg[:, ct], lambda k, ct=ct: par(l, 58 + 2 * k + ct), 3, "sc_dg")

        def ev_copy(dstbuf, key):
            def f(ct, bi, c0, n, bk):
                E.op("act", lambda e: e.activation(out=dstbuf[:, ct, c0:c0 + n], in_=PS[:, bk, 0:n], func=AF.Copy), r=[P(bk)], w=[(key, ct)])
            return f
        proj_group(l, OFF_SC, ev_copy(bgs, "bgs"))
        proj_group(l, OFF_SC + 256, ev_copy(cgs, "cgs"))

        def ev_hx(ct, bi, c0, n, bk):
            dst = ext_views(extp, exts, Hh, ct, bi, c0, n)
            i1 = cgs[:, ct, c0:c0 + n] if bi < 4 else cgs[:, ct, c0:c0 + n].rearrange("p (b j) -> p b j", j=LS)
            i0 = PS[:, bk, 0:n] if bi < 4 else PS[:, bk, 0:n].rearrange("p (b j) -> p b j", j=LS)
            E.op("dve", lambda e: e.tensor_tensor(out=dst, in0=i0, in1=i1, op=ALU.mult), r=[P(bk), ("cgs", ct)], w=["sc_ext"])
        proj_group(l, OFF_SC + 512, ev_hx)
        store_hist(l, Hh, extp, exts, "sc_ext", o_sc_p, o_sc_s)
        for bi, (c0, n) in enumerate(BLKS):
            for ct in range(2):
                bk = bank()
                mm_group(bk, n, conv_taps(3, Hh, dg, extp, exts, ct, bi, c0, n), ["sc_dg", "sc_ext"])
                o_ = ybr[2][:, ct, c0:c0 + n] if bi < 4 else ybr[2][:, ct, c0:c0 + n].rearrange("p (b j) -> p j b", j=LS)
                i0_ = PS[:, bk, 0:n] if bi < 4 else PS[:, bk, 0:n].rearrange("p (j b) -> p j b", b=NS)
                i1_ = bgs[:, ct, c0:c0 + n] if bi < 4 else bgs[:, ct, c0:c0 + n].rearrange("p (b j) -> p j b", j=LS)
                E.op("dve", lambda e: e.tensor_tensor(out=o_, in0=i0_, in1=i1_, op=ALU.mult),
                     r=[P(bk), ("bgs", ct)], w=[("ybr", 2)])
        prefetch_proj(l, 0)
        E.barrier()
        A.release(m_c)
        ybr[0] = A.alloc([2, T], BF16)
        m_c = A.mark()
        Hh = 15
        extp = A.alloc([2, Hh + TP], BF16); exts = A.alloc([2, Hh + LS, NS], BF16)
        wbd = A.alloc([2, 128], F32)
        wbdB = A.alloc([2, 128], BF16)
        pl = A.alloc([2, 16, 128], BF16)
        E.op("dve", lambda e: e.memset(extp[:, :, 0:Hh], 0.0), w=["pl_ext"])
        E.op("dve", lambda e: e.memset(wbd.rearrange("p a b -> p (a b)"), 0.0), w=["wbd"])
        for k in range(4):
            t_, hf = k // 2, k % 2
            ld(wbd[64 * hf:64 * hf + 64, t_, 64 * hf:64 * hf + 64], pool_w[l, k], ["wbd"], chan="lh0", r=["wbd"])
        load_hist(l, st_pool, Hh, exts, "pl_ext")
        ntaps = (4, 16)
        for t_ in range(2):
            E.op("act", lambda e: e.activation(out=wbdB[:, t_, :], in_=wbd[:, t_, :], func=AF.Copy), r=["wbd"], w=["wbdB"])
            for i in range(ntaps[t_]):
                E.op("act", lambda e: e.activation(out=pl[:, t_, i, :], in_=wbd[:, t_, :], func=AF.Copy, scale=poolc[:, t_, i:i + 1]),
                     r=["wbd", "poolc"], w=["pl"])

        def ev_pool(ct, bi, c0, n, bk):
            dst = ext_views(extp, exts, Hh, ct, bi, c0, n)
            i0 = PS[:, bk, 0:n] if bi < 4 else PS[:, bk, 0:n].rearrange("p (b j) -> p b j", j=LS)
            E.op("act", lambda e: e.activation(out=dst, in_=i0, func=AF.Copy), r=[P(bk)], w=["pl_ext"])
        proj_group(l, 0, ev_pool)
        store_hist(l, Hh, extp, exts, "pl_ext", o_pool_p, o_pool_s)

        def pool_steps(ct, bi, c0, n):
            steps = []
            for i in range(ntaps[ct]):
                sh = Hh - i
                rh = extp[:, ct, sh + c0:sh + c0 + n] if bi < 4 else exts[:, ct, sh:sh + LS, :].rearrange("p j b -> p (j b)")
                steps.append((pl[:, ct, i, :], rh))
            return steps
        pt = A.alloc([3, 16], F32)
        for ct in range(2):
            for bi, (c0, n) in enumerate(BLKS):
                bk = bank()
                mm_group(bk, n, pool_steps(ct, bi, c0, n), ["pl", "pl_ext"])
                o_ = ybr[0][:, ct, c0:c0 + n] if bi < 4 else ybr[0][:, ct, c0:c0 + n].rearrange("p (b j) -> p j b", j=LS)
                i_ = PS[:, bk, 0:n] if bi < 4 else PS[:, bk, 0:n].rearrange("p (j b) -> p j b", b=NS)
                E.op("act", lambda e: e.activation(out=o_, in_=i_, func=AF.Copy, scale=par(l, 48 + ct)),
                     r=[P(bk), "parT"], w=[("ybr", 0)])
            bka = bank(); bkx = bank()
            mm_group(bka, 16, pool_steps(ct, 0, 0, 16), ["pl", "pl_ext"])
            mm_group(bkx, 16, [(wbdB[:, ct, :], extp[:, ct, Hh:Hh + 16])], ["wbdB", "pl_ext"])
            E.op("dve", lambda e: e.tensor_tensor(out=pt[:, 0, :], in0=PS[:, bkx, 0:16], in1=poolrm1[:, ct, :], op=ALU.mult), r=[P(bkx), "poolc"], w=["pt0"])
            E.op("dve", lambda e: e.tensor_tensor(out=pt[:, 1, :], in0=PS[:, bka, 0:16], in1=poolr[:, ct, :], op=ALU.mult), r=[P(bka), "poolc"], w=["pt1"])
            E.op("dve", lambda e: e.tensor_tensor(out=pt[:, 2, :], in0=pt[:, 0, :], in1=pt[:, 1, :], op=ALU.add), r=["pt0", "pt1"], w=["pt2"])
            E.op("dve", lambda e: e.tensor_scalar(out=ybr[0][:, ct, 0:16], in0=pt[:, 2, :], scalar1=par(l, 48 + ct), scalar2=None, op0=ALU.mult),
                 r=["pt2", "parT", ("ybr", 0)], w=[("ybr", 0)])
        pre[("gate", l, 0, 0)] = gate_w(l, 0, 0)
        E.barrier()
        A.release(m_c)
        if dbg == "ybr" and l == 0:
            for k in range(4):
                for ct in range(2):
                    m_d = A.mark()
                    dtmp = A.alloc([T], F32)
                    E.op("dve", lambda e: e.tensor_copy(out=dtmp, in_=ybr[k][:, ct, :]), r=[("ybr", k)], w=["dtmp"])
                    E.dma("sp", lambda e: e.dma_start(out=dbg_out[k, ct], in_=dtmp), r=["dtmp"], chan="dbg")
                    E.barrier()
                    A.release(m_d)
        merged = A.alloc([DT, T], BF16)
        m_g = A.mark()
        sgt = A.alloc([2, 512], F32)
        tmpm = A.alloc([2, 512], F32)
        sg_i = 0
        for j2 in range(4):
            for k in range(4):
                sl, wg, wb = gate_w(l, j2, k)
                for jj in range(2):
                    j = 2 * j2 + jj
                    for bi, (c0, n) in enumerate(BLKS):
                        bg_ = bank()
                        mm_group(bg_, n, [(wg[:, kt, jj * 128:(jj + 1) * 128], hB[:, kt, c0:c0 + n]) for kt in range(DT)], [("ws", sl), ("h", bi)])
                        bb_ = bank()
                        mm_group(bb_, n, [(wb[:, kt, jj * 128:(jj + 1) * 128], ybr[k][:, kt, c0:c0 + n]) for kt in range(2)], [("ws", sl), ("ybr", k)])
                        si = sg_i % 2; sg_i += 1
                        E.op("act", lambda e: e.activation(out=sgt[:, si, 0:n], in_=PS[:, bg_, 0:n], func=AF.Sigmoid, bias=par(l, 16 + k * 8 + j)),
                             r=[P(bg_), "parT"], w=[("sgt", si)])
                        mk = ("merged", j, bi)
                        if k == 0:
                            E.op("dve", lambda e: e.tensor_tensor(out=merged[:, j, c0:c0 + n], in0=sgt[:, si, 0:n], in1=PS[:, bb_, 0:n], op=ALU.mult),
                                 r=[("sgt", si), P(bb_)], w=[mk])
                        else:
                            ti = sg_i % 2
                            E.op("dve", lambda e: e.tensor_tensor(out=tmpm[:, ti, 0:n], in0=sgt[:, si, 0:n], in1=PS[:, bb_, 0:n], op=ALU.mult),
                                 r=[("sgt", si), P(bb_)], w=[("tmpm", ti)])
                            E.op("dve", lambda e: e.tensor_tensor(out=merged[:, j, c0:c0 + n], in0=merged[:, j, c0:c0 + n], in1=tmpm[:, ti, 0:n], op=ALU.add),
                                 r=[("tmpm", ti), mk], w=[mk])
        for j2 in range(4):
            sl = wslot()
            wo = wdma(sl, 0, [8, 256], w_out[l][:, j2 * 256:(j2 + 1) * 256].rearrange("(kt p) n -> p kt n", p=128))
            for jj in range(2):
                j = 2 * j2 + jj
                for bi, (c0, n) in enumerate(BLKS):
                    bk = bank()
                    mm_group(bk, n, [(wo[:, kt, jj * 128:(jj + 1) * 128], merged[:, kt, c0:c0 + n]) for kt in range(DT)],
                             [("ws", sl)] + [("merged", kt, bi) for kt in range(DT)])
                    E.op("dve", lambda e: e.tensor_tensor(out=xT[:, j, c0:c0 + n], in0=xT[:, j, c0:c0 + n], in1=PS[:, bk, 0:n], op=ALU.add),
                         r=[P(bk), ("xT", bi)], w=[("xT2", bi, j)])
        E.barrier()
        A.release(m_l)
        sq = A.alloc([2, DT, 512], BF16); rt = A.alloc([2, 2, 512], F32)
        rmsnorm(hB, "h", lambda dt_: par(l, 8 + dt_), "parT", (sq, rt))
        A.release(m_l)
        pre[("ffn", l, 0)] = ffn_w(l, 0)
        E.barrier()
        coop = None
        GS = 11
        if l + 1 < nlayers:
            GS = 6
        ub = A.alloc([GS, T], BF16)
        sgt = A.alloc([3, 512], F32)
        sg_i = 0
        if l + 1 < nlayers:
            coop = Coop(lambda: prep_layer(l + 1))
            E.coop = coop
        groups = [(f0, min(f0 + GS, NFT)) for f0 in range(0, NFT, GS)]
        for (f0, f1) in groups:
            for fi, f in enumerate(range(f0, f1)):
                sl, wg_, wu_ = ffn_w(l, f)
                for bi, (c0, n) in enumerate(BLKS):
                    bg_ = bank()
                    mm_group(bg_, n, [(wg_[:, kt, :], hB[:, kt, c0:c0 + n]) for kt in range(DT)], [("ws", sl), ("h", bi)])
                    bu_ = bank()
                    mm_group(bu_, n, [(wu_[:, kt, :], hB[:, kt, c0:c0 + n]) for kt in range(DT)], [("ws", sl), ("h", bi)])
                    si = sg_i % 3; sg_i += 1
                    E.op("act", lambda e: e.activation(out=sgt[:, si, 0:n], in_=PS[:, bg_, 0:n], func=AF.Silu), r=[P(bg_)], w=[("sgt", si)])
                    E.op("dve", lambda e: e.tensor_tensor(out=ub[:, fi, c0:c0 + n], in0=sgt[:, si, 0:n], in1=PS[:, bu_, 0:n], op=ALU.mult),
                         r=[("sgt", si), P(bu_)], w=[("ub", fi, bi)])
                    if coop is not None:
                        coop.step(2)
            nf = f1 - f0
            for j in range(DT):
                sl = wslot()
                wo = wdma(sl, 0, [nf, 128], w_ffn_out[l][f0 * 128:f1 * 128, j * 128:(j + 1) * 128].rearrange("(kt p) n -> p kt n", p=128))
                for bi, (c0, n) in enumerate(BLKS):
                    bk = bank()
                    mm_group(bk, n, [(wo[:, kt, :], ub[:, kt, c0:c0 + n]) for kt in range(nf)], [("ws", sl)] + [("ub", kt, bi) for kt in range(nf)])
                    E.op("dve", lambda e: e.tensor_tensor(out=xT[:, j, c0:c0 + n], in0=xT[:, j, c0:c0 + n], in1=PS[:, bk, 0:n], op=ALU.add),
                         r=[P(bk), ("xT", bi), ("xT2", bi, j)], w=[("xT2", bi, j)])
                    if coop is not None:
                        coop.step(1)
        if coop is not None:
            coop.finish()
            E.coop = None
        E.barrier()
        for bi in range(5):
            E.lastw[("xT", bi)] = None
        A.release(m_l)

    m0 = A.mark()
    yT = hB
    yTf = xT
    sq = A.alloc([2, DT, 512], BF16)
    rt = A.alloc([2, 2, 512], F32)
    ost = A.alloc([2, D], F32)
    rmsnorm(yTf, "yT", lambda dt_: gfin[:, dt_:dt_ + 1], "gfin", (sq, rt))
    for i in range(T // 128):
        sb = i % 2
        bi = min(i // 4, 4)
        for half in range(2):
            b = bank()
            for q in range(4):
                dt_ = half * 4 + q
                E.op("pe", lambda e: e.transpose(out=PS[:, b, q * 128:(q + 1) * 128], in_=yTf[:, dt_, i * 128:(i + 1) * 128], identity=identF),
                     r=[("yT", bi), "identF"], w=[P(b)], sig=(q == 3))
            if half:
                E.op("act", lambda e: e.activation(out=ost[:, sb, 512:1024], in_=PS[:, b, :], func=AF.Copy), r=[P(b)], w=[("ost", sb, 1)])
            else:
                E.op("dve", lambda e: e.tensor_copy(out=ost[:, sb, 0:512], in_=PS[:, b, :]), r=[P(b)], w=[("ost", sb, 0)])
        E.dma("sp", lambda e: e.dma_start(out=y_out[i * 128:(i + 1) * 128, :], in_=ost[:, sb, :]),
              r=[("ost", sb, 0), ("ost", sb, 1)], chan="st%d" % sb)
    A.release(m0)
    E.final_wait("sp")
    E.emit()
    print("ops", E.nops, "arena peak words", A.peak)
    return nc


_NC_CACHE = {}
_W_NAMES = ["norm1_g", "norm2_g", "final_g", "w_in", "b_gate", "pool_w", "pool_scale", "lam_re", "lam_im", "log_dt",
            "b_re", "b_im", "c_re", "c_im", "d_skip", "w_glu", "b_glu", "sc_w", "cf_w", "cf_ln_g", "cf_ln_b",
            "w_branch", "w_out", "w_ffn_in", "w_ffn_out"]


def kernel(**inputs):
    f = lambda a: np.ascontiguousarray(np.asarray(a, dtype=np.float32))
    x_prompt = f(inputs["x_prompt"]); x_sample = f(inputs["x_sample"])
    if "nc" not in _NC_CACHE:
        _NC_CACHE["nc"] = build()
    nc = _NC_CACHE["nc"]
    consts = host_consts()
    wts = {k: f(inputs[k]) for k in _W_NAMES}
    sp = f(inputs["state_pool"]); sr = f(inputs["state_ssm_re"]); si = f(inputs["state_ssm_im"])
    ssc = f(inputs["state_shortconv"]); scf = f(inputs["state_conformer"])
    in_maps = []
    for c in range(NCORES):
        b0, b1 = c * NS, (c + 1) * NS
        m = dict(wts)
        m.update(consts)
        m["xin"] = np.ascontiguousarray(np.concatenate([x_prompt[c], x_sample[b0:b1].reshape(NS * LS, D)], axis=0))
        m["st_pool"] = np.ascontiguousarray(sp[:, b0:b1].reshape(DEPTH, NS * 15, C))
        m["st_re"] = np.ascontiguousarray(sr[:, b0:b1].reshape(DEPTH, NS, 1024))
        m["st_im"] = np.ascontiguousarray(si[:, b0:b1].reshape(DEPTH, NS, 1024))
        m["st_sc"] = np.ascontiguousarray(ssc[:, b0:b1].reshape(DEPTH, NS * 2, C))
        m["st_cf"] = np.ascontiguousarray(scf[:, b0:b1].reshape(DEPTH, NS * 30, C))
        in_maps.append(m)
    res = run_bass_kernel_spmd(nc, in_maps, core_ids=list(range(NCORES)))
    R = res.results
    cat1 = lambda name, shp: np.concatenate([r[name].reshape(shp) for r in R], axis=1)
    stk1 = lambda name, shp: np.stack([r[name].reshape(shp) for r in R], axis=1)
    y_prompt = np.stack([r["y"][:TP] for r in R], axis=0)
    y_sample = np.concatenate([r["y"][TP:].reshape(NS, LS, D) for r in R], axis=0)
    outs = (y_prompt, y_sample,
            stk1("pool_p", (DEPTH, 15, C)), cat1("pool_s", (DEPTH, NS, 15, C)),
            stk1("re_p", (DEPTH, 16, 64)), cat1("re_s", (DEPTH, NS, 16, 64)),
            stk1("im_p", (DEPTH, 16, 64)), cat1("im_s", (DEPTH, NS, 16, 64)),
            stk1("sc_p", (DEPTH, 2, C)), cat1("sc_s", (DEPTH, NS, 2, C)),
            stk1("cf_p", (DEPTH, 30, C)), cat1("cf_s", (DEPTH, NS, 30, C)))
    return tuple(np.ascontiguousarray(o.astype(np.float32)) for o in outs)
```

```python
import numpy as np
import concourse.bass as bass
import concourse.mybir as mybir
from concourse.ap import AP
from concourse.bass_utils import run_bass_kernel_spmd

F32 = mybir.dt.float32
BF16 = mybir.dt.bfloat16
I32 = mybir.dt.int32
AF = mybir.ActivationFunctionType
ALU = mybir.AluOpType

NCORES = 8
D = 1024
DT = 8
DEPTH = 4
TP = 2048
NS = 16
LS = 8
T = TP + NS * LS
C = 256
IN_COLS = 5888
OFF_SSM, OFF_SC, OFF_CF, OFF_GATE = 256, 512, 1280, 1792
FF = 2816
NFT = 22
BLKS = [(0, 512), (512, 512), (1024, 512), (1536, 512), (2048, 128)]
NCH = T // 8
USE_ARS = True
TWO_PI = 6.283185307179586


class _Rec:
    def __init__(self):
        self.call = None

    def __getattr__(self, name):
        def f(*a, **k):
            self.call = (name, a, k)
            return self
        return f


def _freeze(fn):
    if fn is None:
        return None
    r = _Rec()
    fn(r)
    name, a, k = r.call
    return lambda e: getattr(e, name)(*a, **k)


class Coop:
    def __init__(self, fn):
        import threading
        self.go = threading.Semaphore(0)
        self.done = threading.Semaphore(0)
        self.finished = False
        self.exc = None

        def run():
            try:
                self.go.acquire()
                fn()
            except BaseException as ex:
                self.exc = ex
            self.finished = True
            self.done.release()
        self.t = threading.Thread(target=run, daemon=True)
        self.t.start()

    def inside(self):
        import threading
        return threading.current_thread() is self.t

    def tick(self):
        self.done.release()
        self.go.acquire()

    def step(self, n):
        for _ in range(n):
            if self.finished:
                break
            self.go.release()
            self.done.acquire()
        if self.exc is not None:
            raise self.exc

    def finish(self):
        while not self.finished:
            self.go.release()
            self.done.acquire()
        if self.exc is not None:
            raise self.exc


class Emit:
    COMPUTE = ("pe", "act", "dve", "pool")
    coop = None

    def __init__(self, nc, nchan=24):
        self.nc = nc
        self.eng = {"pe": nc.tensor, "act": nc.scalar, "dve": nc.vector, "pool": nc.gpsimd, "sp": nc.sync}
        self.lists = {e: [] for e in self.eng}
        self.ticks = {e: 0 for e in self.COMPUTE}
        self.pending = {e: False for e in self.COMPUTE}
        self.waited = {e: {} for e in self.eng}
        self.lastw = {}
        self.readers = {}
        self.chanv = {}
        self.nops = 0

    def _need(self, engname, ev, waits):
        if ev is None:
            return
        sem, val, src = ev
        if src == engname and engname == "pe":
            return
        if self.waited[engname].get(sem, 0) >= val:
            return
        waits.append((sem, val))

    def _deps(self, engname, reads, writes, same_eng_war=False):
        waits = []
        for r in reads:
            self._need(engname, self.lastw.get(r), waits)
        for w in writes:
            self._need(engname, self.lastw.get(w), waits)
            for ev in self.readers.get(w, ()):
                self._need(engname, ev, waits)
        best = {}
        for s, v in waits:
            best[s] = max(best.get(s, 0), v)
        for s, v in best.items():
            self.waited[engname][s] = max(self.waited[engname].get(s, 0), v)
        return list(best.items())

    def _commit(self, ev, reads, writes):
        for r in reads:
            self.readers.setdefault(r, []).append(ev)
        for w in writes:
            self.lastw[w] = ev
            self.readers[w] = []

    def op(self, engname, fn, r=(), w=(), sig=True):
        waits = self._deps(engname, r, w)
        if sig:
            self.ticks[engname] += 1
            ev = (engname, self.ticks[engname], engname)
        else:
            ev = (engname, self.ticks[engname] + 1, engname)
        self.pending[engname] = not sig
        self.lists[engname].append((_freeze(fn), waits, (engname, 1) if sig else None))
        self._commit(ev, r, w)
        self.nops += 1
        if self.coop is not None and self.coop.inside():
            self.coop.tick()

    def dma(self, queue, fn, r=(), w=(), chan="c0"):
        waits = self._deps(queue, r, w, same_eng_war=True)
        prev = self.chanv.get(chan, 0)
        if prev > self.waited[queue].get(chan, 0):
            waits = [(s_, v_) for s_, v_ in waits if s_ != chan] + [(chan, prev)]
            self.waited[queue][chan] = prev
        self.chanv[chan] = self.chanv.get(chan, 0) + 16
        ev = (chan, self.chanv[chan], "dma")
        self.lists[queue].append((_freeze(fn), waits, (chan, 16)))
        self._commit(ev, r, w)
        self.nops += 1
        if self.coop is not None and self.coop.inside():
            self.coop.tick()

    def barrier(self):
        for e in self.COMPUTE:
            assert not self.pending[e], f"engine {e} has unsignalled tail op at barrier"
        for e in self.eng:
            if e == "pe":
                continue
            waits = []
            for o in self.COMPUTE:
                if o != e and self.ticks[o] > self.waited[e].get(o, 0):
                    waits.append((o, self.ticks[o]))
                    self.waited[e][o] = self.ticks[o]
            for c, v in self.chanv.items():
                if v > self.waited[e].get(c, 0):
                    waits.append((c, v))
                    self.waited[e][c] = v
            if waits:
                self.lists[e].append((None, waits, None))

    def final_wait(self, engname="sp"):
        waits = [(c, v) for c, v in self.chanv.items()]
        self.lists[engname].append((None, waits, None))

    def emit(self):
        nc = self.nc
        from contextlib import ExitStack
        with ExitStack() as es:
            sems = {}
            for name in list(self.COMPUTE) + sorted(self.chanv.keys()):
                sems[name] = es.enter_context(nc.semaphore("s_" + name))
            block = es.enter_context(nc.Block())

            def run(engname):
                def body(e):
                    for fn, waits, inc in self.lists[engname]:
                        for s, v in waits:
                            e.wait_ge(sems[s], v)
                        if fn is not None:
                            ins = fn(e)
                            if inc is not None:
                                ins.then_inc(sems[inc[0]], inc[1])
                return body

            block.tensor(run("pe"))
            block.scalar(run("act"))
            block.vector(run("dve"))
            block.gpsimd(run("pool"))
            block.sync(run("sp"))


class Arena:
    def __init__(self, nc, words):
        self.t = nc.alloc_sbuf_tensor("arena", [128, words], F32)
        self.words = words
        self.top = 0
        self.peak = 0

    def mark(self):
        return self.top

    def release(self, m):
        self.top = m

    def alloc(self, shape, dtype):
        n = 1
        for s in shape:
            n *= s
        bpe = 4 if dtype in (F32, I32) else 2
        w = (n * bpe + 3) // 4
        w = (w + 7) // 8 * 8
        off = self.top
        self.top += w
        self.peak = max(self.peak, self.top)
        assert self.top <= self.words, f"arena overflow {self.top} > {self.words}"
        v = self.t.ap()[:, off:off + w]
        if dtype != F32:
            v = v.bitcast(dtype)
        v = v[:, 0:n]
        if len(shape) == 2:
            v = v.rearrange("p (a b) -> p a b", a=shape[0])
        elif len(shape) == 3:
            v = v.rearrange("p (a b c) -> p a b c", a=shape[0], b=shape[1])
        elif len(shape) == 4:
            v = v.rearrange("p (a b c d) -> p a b c d", a=shape[0], b=shape[1], c=shape[2])
        return v


def fv(ap, dims, off=0):
    return AP(ap.tensor, ap.offset + off, [list(ap.ap[0])] + [list(d) for d in dims])


def host_consts():
    c = {}
    c["ident"] = np.eye(128, dtype=np.float32)
    kv = np.array(list(range(7, -1, -1)) + list(range(-7, 1)) + list(range(1, 9)), dtype=np.float32)
    c["kv"] = np.tile(kv[None, :], (128, 1))
    j = np.arange(128) // 16
    c["maskc"] = (j[:, None] >= np.arange(8)[None, :]).astype(np.float32)
    pm = np.zeros((128, 2, 8, 128), np.float32)
    for k in range(128):
        e = (k // 16) % 2
        for g in range(8):
            pm[k, e, g, g * 16 + (k % 16)] = 1.0
    c["pmat"] = pm
    wins = (2, 4, 8, 16)
    coef = np.zeros((128, 2, 16), np.float32)
    r = np.ones((128, 2, 16), np.float32)
    for p in range(128):
        for t in range(2):
            win = wins[2 * t + (1 if p >= 64 else 0)]
            for i in range(16):
                if i < win:
                    coef[p, t, i] = 1.0 / win - (1.0 if i == 0 else 0.0)
                r[p, t, i] = win / min(i + 1, win)
    c["poolc"] = coef
    c["poolr"] = r
    c["poolrm1"] = r - 1.0
    return c


def build(nlayers=DEPTH, dbg=None):
    nc = bass.Bass("TRN2", target_bir_lowering=False)
    E = Emit(nc)

    def din(name, shape, dt=F32):
        return nc.dram_tensor(name, list(shape), dt, kind="ExternalInput").ap()

    def dout(name, shape, dt=F32):
        return nc.dram_tensor(name, list(shape), dt, kind="ExternalOutput").ap()

    L4 = DEPTH
    xin = din("xin", [T, D])
    st_pool = din("st_pool", [L4, NS * 15, C]); st_re = din("st_re", [L4, NS, 1024]); st_im = din("st_im", [L4, NS, 1024])
    st_sc = din("st_sc", [L4, NS * 2, C]); st_cf = din("st_cf", [L4, NS * 30, C])
    norm1_g = din("norm1_g", [L4, D]); norm2_g = din("norm2_g", [L4, D]); final_g = din("final_g", [D])
    w_in = din("w_in", [L4, D, IN_COLS]); b_gate = din("b_gate", [L4, 4, D])
    pool_w = din("pool_w", [L4, 4, 64, 64]); pool_scale = din("pool_scale", [L4, C])
    lam_re = din("lam_re", [L4, 16, 64]); lam_im = din("lam_im", [L4, 16, 64]); log_dt = din("log_dt", [L4, 16])
    b_re = din("b_re", [L4, 16, 64, 16]); b_im = din("b_im", [L4, 16, 64, 16])
    c_re = din("c_re", [L4, 16, 16, 64]); c_im = din("c_im", [L4, 16, 16, 64])
    d_skip = din("d_skip", [L4, C]); w_glu = din("w_glu", [L4, C, C]); b_glu = din("b_glu", [L4, C])
    sc_w = din("sc_w", [L4, 3, C]); cf_w = din("cf_w", [L4, 31, C]); cf_ln_g = din("cf_ln_g", [L4, C]); cf_ln_b = din("cf_ln_b", [L4, C])
    w_branch = din("w_branch", [L4, 4, C, D]); w_out = din("w_out", [L4, D, D])
    w_ffn_in = din("w_ffn_in", [L4, D, 2 * FF]); w_ffn_out = din("w_ffn_out", [L4, FF, D])
    ident_d = din("ident", [128, 128]); kv_d = din("kv", [128, 24]); maskc_d = din("maskc", [128, 8])
    pmat_d = din("pmat", [128, 2, 8, 128]); poolc_d = din("poolc", [128, 2, 16]); poolr_d = din("poolr", [128, 2, 16])
    poolrm1_d = din("poolrm1", [128, 2, 16])

    y_out = dout("y", [T, D])
    o_pool_p = dout("pool_p", [L4, 15, C]); o_pool_s = dout("pool_s", [L4, NS * 15, C])
    o_re_p = dout("re_p", [L4, 8, 128]); o_re_s = dout("re_s", [L4, NS, 1024])
    o_im_p = dout("im_p", [L4, 8, 128]); o_im_s = dout("im_s", [L4, NS, 1024])
    o_sc_p = dout("sc_p", [L4, 2, C]); o_sc_s = dout("sc_s", [L4, NS * 2, C])
    o_cf_p = dout("cf_p", [L4, 30, C]); o_cf_s = dout("cf_s", [L4, NS * 30, C])
    dbg_out = dout("dbg", [4, 2, 128, T]) if dbg else None
    ssmw_d = nc.dram_tensor("ssmw_scr", [L4, 128, 10240], BF16, kind="Internal").ap()

    A = Arena(nc, 52736)
    psum_t = nc.alloc_psum_tensor("ps", [128, 8, 512], F32)
    PS = psum_t.ap()
    PSB = PS.rearrange("p b n -> p (b n)").bitcast(BF16).rearrange("p (b n) -> p b n", b=8)
    bank_ctr = [0]

    held = set()

    def bank():
        if E.coop is not None:
            if E.coop.inside():
                return 7
        while True:
            b = bank_ctr[0] % 8
            bank_ctr[0] += 1
            if b in held or (E.coop is not None and b == 7):
                continue
            return b

    def P(b):
        return ("ps", b)

    xT = A.alloc([DT, T], F32)
    hB = A.alloc([DT, T], BF16)
    identF = A.alloc([128], F32)
    identB = A.alloc([128], BF16)
    onesB = A.alloc([128], BF16)
    cst = A.alloc([8], F32)
    gfin = A.alloc([DT], F32)
    parT = A.alloc([L4, 128], F32)
    scoef = A.alloc([L4, 8, 8, 3], F32)
    maskc = A.alloc([8], F32)
    poolc = A.alloc([2, 16], F32); poolr = A.alloc([2, 16], F32); poolrm1 = A.alloc([2, 16], F32)
    ws = A.alloc([3, 2560], BF16)
    ws_ctr = [0]

    ld_rr = [0]

    def ld(out, in_, w, chan=None, slow=False, q="sp", r=()):
        if chan is None:
            chan = "ld%d" % (ld_rr[0] % 8)
            ld_rr[0] += 1
        if slow:
            E.dma(q, lambda e: e.dma_start(out=out, in_=in_, allow_slow_non_contiguous=True), w=w, r=r, chan=chan)
        else:
            E.dma(q, lambda e: e.dma_start(out=out, in_=in_), w=w, r=r, chan=chan)

    ld(identF, ident_d, ["identF"])
    ld(identB, ident_d, ["identB"], chan="ldp", q="pool")
    ld(maskc, maskc_d, ["maskc"]); ld(poolc, poolc_d, ["poolc"]); ld(poolr, poolr_d, ["poolc"]); ld(poolrm1, poolrm1_d, ["poolc"])
    E.op("dve", lambda e: e.memset(onesB, 1.0), w=["onesB"])
    E.op("dve", lambda e: e.memset(cst[:, 0:1], 1e-6), w=["cst"])
    E.op("dve", lambda e: e.memset(cst[:, 1:2], 1e-5), w=["cst"])
    E.op("dve", lambda e: e.memset(cst[:, 2:3], 0.0), w=["cst"])
    ld(gfin, final_g.rearrange("(t p) -> p t", p=128), ["gfin"], slow=True)

    def wslot():
        i = ws_ctr[0] % 3
        ws_ctr[0] += 1
        return i

    def wdma(slot, off, shape, src):
        n = 1
        for s in shape:
            n *= s
        v = ws[:, slot, off:off + n]
        if len(shape) == 2:
            v = v.rearrange("p (a b) -> p a b", a=shape[0])
        elif len(shape) == 3:
            v = v.rearrange("p (a b c) -> p a b c", a=shape[0], b=shape[1])
        E.dma("pool", lambda e: e.dma_start(out=v, in_=src), w=[("ws", slot)], chan="w%d" % slot)
        return v

    m0 = A.mark()
    stg = A.alloc([4, D], F32)
    for i in range(T // 128):
        sb = i % 4
        E.dma("sp", lambda e, i=i, sb=sb: e.dma_start(out=stg[:, sb, :], in_=xin[i * 128:(i + 1) * 128, :]),
              w=[("stg", sb)], chan="ldx%d" % sb)
        for half in range(2):
            b = bank()
            for q in range(4):
                dt_ = half * 4 + q
                E.op("pe", lambda e, b=b, q=q, dt_=dt_, sb=sb: e.transpose(
                    out=PS[:, b, q * 128:(q + 1) * 128], in_=stg[:, sb, dt_ * 128:(dt_ + 1) * 128],
                    identity=identF), r=[("stg", sb), "identF"], w=[P(b)], sig=(q == 3))
            if half:
                E.op("act", lambda e, b=b, half=half, i=i: e.activation(
                    out=xT[:, half * 4:half * 4 + 4, i * 128:(i + 1) * 128],
                    in_=PS[:, b, :].rearrange("p (a c) -> p a c", a=4), func=AF.Copy), r=[P(b)], w=[("xT", i // 4 if i < 16 else 4)])
            else:
                E.op("dve", lambda e, b=b, half=half, i=i: e.tensor_copy(
                    out=xT[:, half * 4:half * 4 + 4, i * 128:(i + 1) * 128],
                    in_=PS[:, b, :].rearrange("p (a c) -> p a c", a=4)), r=[P(b)], w=[("xT", i // 4 if i < 16 else 4)])
    for l in range(nlayers):
        prow = stg[:, l % 2, 0:128]
        srcs = [(0, norm1_g[l].rearrange("(r p) -> r p", p=128)), (8, norm2_g[l].rearrange("(r p) -> r p", p=128)),
                (16, b_gate[l].rearrange("k (r p) -> (k r) p", p=128)), (48, pool_scale[l].rearrange("(r p) -> r p", p=128)),
                (50, b_glu[l].rearrange("(r p) -> r p", p=128)), (52, cf_ln_g[l].rearrange("(r p) -> r p", p=128)),
                (54, cf_ln_b[l].rearrange("(r p) -> r p", p=128)), (56, d_skip[l].rearrange("(r p) -> r p", p=128)),
                (58, sc_w[l].rearrange("k (r p) -> (k r) p", p=128)), (64, cf_w[l].rearrange("k (r p) -> (k r) p", p=128))]
        keys = [("prow", l % 2, r0) for r0, _ in srcs]
        E.op("dve", lambda e, prow=prow: e.memset(prow, 0.0), r=[], w=[("stg", l % 2)] + keys)
        for (r0, s_), k_ in zip(srcs, keys):
            nr = s_.shape[0]
            ld(prow[r0:r0 + nr, :], s_, [k_], chan="ldx%d" % (l % 2))
        b = bank()
        E.op("pe", lambda e, b=b, prow=prow: e.transpose(out=PS[:, b, 0:128], in_=prow, identity=identF),
             r=keys + ["identF"], w=[P(b)])
        E.op("dve", lambda e, b=b, l=l: e.tensor_copy(out=parT[:, l, :], in_=PS[:, b, 0:128]), r=[P(b)], w=["parT"])
        E.op("dve", lambda e: e.memset(cst[:, 3:4], 0.0), r=keys, w=[("stg", l % 2)])
    E.barrier()
    A.release(m0)

    def par(l, r0, n=1):
        return parT[:, l, r0:r0 + n]
    kvt = A.alloc([24], F32)
    ld(kvt, kv_d, ["kvt"])

    def prep_layer(l):
        WbT = A.alloc([2, 8, 8, 32], BF16)
        WaTs = A.alloc([2, 2, 8, 128], BF16)
        WbT2 = A.alloc([2, 8, 8, 32], BF16)
        E.op("dve", lambda e: e.memset(WbT2.rearrange("p a b c d -> p (a b c d)"), 0.0), w=["WbT"])
        E.op("dve", lambda e: e.memset(WbT.rearrange("p a b c d -> p (a b c d)"), 0.0), w=["WbT"])
        E.op("dve", lambda e: e.memset(WaTs.rearrange("p a b c d -> p (a b c d)"), 0.0), w=["WaTs"])
        lamr = A.alloc([8], F32); lami = A.alloc([8], F32); ldt = A.alloc([8], F32)
        bre = A.alloc([8, 16], F32); bim = A.alloc([8, 16], F32); cre = A.alloc([8, 16], F32); cim = A.alloc([8, 16], F32)
        dcol = A.alloc([16], F32)
        ld(lamr, AP(lam_re.tensor, l * 1024, [[1, 128], [128, 8]]), [("lamr", l)], slow=True)
        ld(lami, AP(lam_im.tensor, l * 1024, [[1, 128], [128, 8]]), [("lami", l)], slow=True)
        for e_ in range(2):
            ld(ldt[64 * e_:64 * e_ + 64, :], AP(log_dt.tensor, l * 16 + e_, [[0, 64], [2, 8]]), [("ldt", l, e_)], slow=True)
            for q_ in range(8):
                ld(cre[64 * e_:64 * e_ + 64, q_, :], AP(c_re.tensor, l * 16384 + e_ * 1024 + q_ * 2048, [[1, 64], [64, 16]]), [("cre", l, e_, q_)], slow=True)
                ld(cim[64 * e_:64 * e_ + 64, q_, :], AP(c_im.tensor, l * 16384 + e_ * 1024 + q_ * 2048, [[1, 64], [64, 16]]), [("cim", l, e_, q_)], slow=True)
        ld(bre, AP(b_re.tensor, l * 16384, [[16, 128], [2048, 8], [1, 16]]), [("bre", l)], slow=True)
        ld(bim, AP(b_im.tensor, l * 16384, [[16, 128], [2048, 8], [1, 16]]), [("bim", l)], slow=True)
        for j_ in range(8):
            ld(dcol[16 * j_:16 * j_ + 16, :], AP(d_skip.tensor, l * 256, [[1, 16], [16, 16]]), [("dcol", l, j_)], slow=True)
        dck = [("dcol", l, j_) for j_ in range(8)]

        dtv = A.alloc([8], F32); ar = A.alloc([8], F32); ai = A.alloc([8], F32)
        t24a = A.alloc([8, 24], F32); t24b = A.alloc([8, 24], F32); t24i = A.alloc([8, 24], I32)
        Er = A.alloc([8, 24], F32); Ei = A.alloc([8, 24], F32); mag = A.alloc([8, 24], F32)
        E.op("act", lambda e: e.activation(out=dtv, in_=ldt, func=AF.Exp), r=[("ldt", l, 0), ("ldt", l, 1)], w=["dtv"])
        E.op("dve", lambda e: e.tensor_tensor(out=ar, in0=lamr, in1=dtv, op=ALU.mult), r=[("lamr", l), "dtv"], w=["ar"])
        E.op("dve", lambda e: e.tensor_tensor(out=ai, in0=lami, in1=dtv, op=ALU.mult), r=[("lami", l), "dtv"], w=["ai"])
        kvb = fv(kvt, [[0, 8], [1, 24]])
        E.op("dve", lambda e: e.tensor_tensor(out=t24a, in0=fv(ar, [[1, 8], [0, 24]]), in1=kvb, op=ALU.mult), r=["ar", "kvt"], w=["t24a"])
        E.op("act", lambda e: e.activation(out=mag, in_=t24a, func=AF.Exp), r=["t24a"], w=["mag"])
        E.op("dve", lambda e: e.tensor_tensor(out=t24a, in0=fv(ai, [[1, 8], [0, 24]]), in1=kvb, op=ALU.mult), r=["ai", "kvt", "mag"], w=["t24a"])
        for which, dst in ((0, Ei), (1, Er)):
            E.op("dve", lambda e, which=which: e.tensor_scalar(out=t24b, in0=t24a, scalar1=1.0 / TWO_PI, scalar2=0.25 * which,
                                                                 op0=ALU.mult, op1=ALU.add), r=["t24a"], w=["t24b"])
            E.op("dve", lambda e: e.tensor_copy(out=t24i, in_=t24b), r=["t24b"], w=["t24i"])
            E.op("dve", lambda e, dst=dst: e.tensor_copy(out=dst, in_=t24i), r=["t24i"], w=[("E", which)])
            E.op("dve", lambda e, dst=dst: e.tensor_tensor(out=t24b, in0=t24b, in1=dst, op=ALU.subtract), r=["t24b", ("E", which)], w=["t24b"])
            E.op("act", lambda e, dst=dst: e.activation(out=dst, in_=t24b, func=AF.Sin, scale=TWO_PI * (1.0 - 2e-6)), r=["t24b"], w=[("E", which)])
            E.op("dve", lambda e, dst=dst: e.tensor_tensor(out=dst, in0=dst, in1=mag, op=ALU.mult), r=[("E", which), "mag"], w=[("E", which)])
        EK = [("E", 0), ("E", 1)]
        sm = A.alloc([8, 8], F32)
        def S_(i):
            return fv(sm, [[8, 8]], off=i)
        a_r = fv(Er, [[24, 8]], off=16); a_i = fv(Ei, [[24, 8]], off=16)
        seq = [
            lambda e: e.tensor_scalar(out=S_(0), in0=a_r, scalar1=-1.0, scalar2=None, op0=ALU.add),
            lambda e: e.tensor_tensor(out=S_(1), in0=lamr, in1=lamr, op=ALU.mult),
            lambda e: e.tensor_tensor(out=S_(2), in0=lami, in1=lami, op=ALU.mult),
            lambda e: e.tensor_tensor(out=S_(1), in0=S_(1), in1=S_(2), op=ALU.add),
            lambda e: e.reciprocal(out=S_(1), in_=S_(1)),
            lambda e: e.tensor_tensor(out=S_(2), in0=S_(0), in1=lamr, op=ALU.mult),
            lambda e: e.tensor_tensor(out=S_(3), in0=a_i, in1=lami, op=ALU.mult),
            lambda e: e.tensor_tensor(out=S_(2), in0=S_(2), in1=S_(3), op=ALU.add),
            lambda e: e.tensor_tensor(out=S_(4), in0=S_(2), in1=S_(1), op=ALU.mult),
            lambda e: e.tensor_tensor(out=S_(2), in0=a_i, in1=lamr, op=ALU.mult),
            lambda e: e.tensor_tensor(out=S_(3), in0=S_(0), in1=lami, op=ALU.mult),
            lambda e: e.tensor_tensor(out=S_(2), in0=S_(2), in1=S_(3), op=ALU.subtract),
            lambda e: e.tensor_tensor(out=S_(5), in0=S_(2), in1=S_(1), op=ALU.mult),
        ]
        for fn in seq:
            E.op("dve", fn, r=EK + [("lamr", l), ("lami", l), "sm"], w=["sm"])
        Bbr = A.alloc([8, 16], F32); Bbi = A.alloc([8, 16], F32); tq = A.alloc([8, 16], F32)
        krb = fv(sm, [[8, 8], [0, 16]], off=4); kib = fv(sm, [[8, 8], [0, 16]], off=5)
        seq = [
            (lambda e: e.tensor_tensor(out=Bbr, in0=krb, in1=bre, op=ALU.mult), ["Bbr"]),
            (lambda e: e.tensor_tensor(out=tq, in0=kib, in1=bim, op=ALU.mult), ["tq"]),
            (lambda e: e.tensor_tensor(out=Bbr, in0=Bbr, in1=tq, op=ALU.subtract), ["Bbr"]),
            (lambda e: e.tensor_tensor(out=Bbi, in0=krb, in1=bim, op=ALU.mult), ["Bbi"]),
            (lambda e: e.tensor_tensor(out=tq, in0=kib, in1=bre, op=ALU.mult), ["tq"]),
            (lambda e: e.tensor_tensor(out=Bbi, in0=Bbi, in1=tq, op=ALU.add), ["Bbi"]),
        ]
        for fn, w_ in seq:
            E.op("dve", fn, r=["sm", ("bre", l), ("bim", l), "Bbr", "Bbi", "tq"], w=w_)
        P1 = A.alloc([8, 8, 16], F32); P2 = A.alloc([8, 8, 16], F32)

        def tab(Ex, base):
            return fv(Ex, [[24, 8], [1, 8], [0, 16]], off=base)

        def vec(Vx):
            return fv(Vx, [[16, 8], [0, 8], [1, 16]])

        def cplx(base, Vr, Vi, vkeys, out_re, out_im, neg_im, okeys):
            E.op("dve", lambda e: e.tensor_tensor(out=P1, in0=tab(Er, base), in1=vec(Vr), op=ALU.mult), r=EK + vkeys + ["P2"], w=["P1"])
            E.op("dve", lambda e: e.tensor_tensor(out=P2, in0=tab(Ei, base), in1=vec(Vi), op=ALU.mult), r=EK + vkeys + ["P1"], w=["P2"])
            out_re(ALU.subtract, okeys)
            E.op("dve", lambda e: e.tensor_tensor(out=P1, in0=tab(Er, base), in1=vec(Vi), op=ALU.mult), r=EK + vkeys + ["P2"] + okeys, w=["P1"])
            E.op("dve", lambda e: e.tensor_tensor(out=P2, in0=tab(Ei, base), in1=vec(Vr), op=ALU.mult), r=EK + vkeys + ["P1"] + okeys, w=["P2"])
            out_im(neg_im, okeys)

        def wbt_out(ri):
            def f(op_or_neg, okeys):
                op = op_or_neg if ri == 0 else ALU.add
                for e_ in range(2):
                    E.op("dve", lambda e, e_=e_: e.tensor_tensor(
                        out=WbT[64 * e_:64 * e_ + 64, ri, :, :, 16 * e_:16 * e_ + 16],
                        in0=P1[64 * e_:64 * e_ + 64], in1=P2[64 * e_:64 * e_ + 64], op=op), r=["P1", "P2"], w=["WbT", ("WbTl", l)])
                    E.op("dve", lambda e, e_=e_: e.tensor_tensor(
                        out=WbT2[64 * e_:64 * e_ + 64, ri, :, :, 16 * e_:16 * e_ + 16].rearrange("p s q c -> p q s c"),
                        in0=P1[64 * e_:64 * e_ + 64], in1=P2[64 * e_:64 * e_ + 64], op=op), r=["P1", "P2"], w=["WbT", ("WbTl", l)])
            return f
        cplx(0, Bbr, Bbi, ["Bbr", "Bbi"], wbt_out(0), wbt_out(1), False, ["WbT"])

        Gr = A.alloc([8, 128], BF16); nGi = A.alloc([8, 128], BF16)

        def gq_out(dst, is_im, key):
            dv = dst.rearrange("p q (j c) -> p q j c", j=8)
            def f(op_or_neg, okeys):
                if not is_im:
                    E.op("dve", lambda e: e.tensor_tensor(out=dv, in0=P1, in1=P2, op=ALU.subtract), r=["P1", "P2"], w=[key])
                else:
                    E.op("dve", lambda e: e.scalar_tensor_tensor(out=dv, in0=P1, scalar=-1.0, in1=P2, op0=ALU.mult, op1=ALU.subtract),
                         r=["P1", "P2"], w=[key])
            return f
        ck = [(n_, l, e_, q_) for n_ in ("cre", "cim") for e_ in range(2) for q_ in range(8)]
        cplx(8, cre, cim, ck, gq_out(Gr, False, "Gr"), gq_out(nGi, True, "nGi"), True, ["Gr", "nGi"])
        sc_l = scoef[:, l]
        def cf_(lev, c):
            return fv(sc_l, [[24, 8]], off=lev * 3 + c)
        E.op("dve", lambda e: e.tensor_copy(out=cf_(0, 0), in_=fv(Er, [[24, 8]], off=23)), r=EK, w=["scoef"])
        E.op("dve", lambda e: e.tensor_copy(out=cf_(0, 1), in_=fv(Ei, [[24, 8]], off=23)), r=EK, w=["scoef"])
        for lev in range(7):
            sq_ = [
                lambda e, lev=lev: e.tensor_tensor(out=S_(6), in0=cf_(lev, 0), in1=cf_(lev, 0), op=ALU.mult),
                lambda e, lev=lev: e.tensor_tensor(out=S_(7), in0=cf_(lev, 1), in1=cf_(lev, 1), op=ALU.mult),
                lambda e, lev=lev: e.tensor_tensor(out=cf_(lev + 1, 0), in0=S_(6), in1=S_(7), op=ALU.subtract),
                lambda e, lev=lev: e.tensor_tensor(out=S_(6), in0=cf_(lev, 0), in1=cf_(lev, 1), op=ALU.mult),
                lambda e, lev=lev: e.tensor_scalar(out=cf_(lev + 1, 1), in0=S_(6), scalar1=2.0, scalar2=None, op0=ALU.mult),
            ]
            for fn in sq_:
                E.op("dve", fn, r=["scoef", "sm"], w=["scoef", "sm"])
        for lev in range(8):
            E.op("dve", lambda e, lev=lev: e.tensor_scalar(out=cf_(lev, 2), in0=cf_(lev, 1), scalar1=-1.0, scalar2=None, op0=ALU.mult),
                 r=["scoef"], w=["scoef"])
        tmk = A.alloc([8, 16], F32)
        idv = fv(identF, [[16, 8], [1, 16]])
        for q in range(8):
            tile_, q4 = q // 4, q % 4
            b = bank()
            E.op("pe", lambda e, q=q, b=b: e.matmul(PS[:, b, 0:256], lhsT=Gr[:, q, :], rhs=WbT[:, 0, q].rearrange("p s c -> p (s c)"),
                                                      start=True, stop=False), r=["Gr", ("WbTl", l)], w=[P(b)], sig=False)
            E.op("pe", lambda e, q=q, b=b: e.matmul(PS[:, b, 0:256], lhsT=nGi[:, q, :], rhs=WbT[:, 1, q].rearrange("p s c -> p (s c)"),
                                                      start=False, stop=True), r=["nGi", ("WbTl", l)], w=[P(b)])
            for es in range(2):
                g = 2 * q + es
                psv = fv(PS[:, b, 0:256], [[32, 8], [1, 16]], off=16 * es)
                E.op("dve", lambda e, psv=psv: e.tensor_tensor(out=tmk, in0=psv, in1=fv(maskc, [[1, 8], [0, 16]]), op=ALU.mult),
                     r=[P(b), "maskc"], w=["tmk"])
                outv = fv(WaTs[:, tile_, es], [[128, 8], [1, 16]], off=q4 * 32 + 16 * es)
                E.op("dve", lambda e, outv=outv, g=g: e.scalar_tensor_tensor(out=outv, in0=idv, scalar=dcol[:, g:g + 1], in1=tmk,
                                                                               op0=ALU.mult, op1=ALU.add),
                     r=["tmk", "identF"] + dck, w=["WaTs", ("WaTsl", l)])
        wst = A.alloc([2, 1024], BF16)
        wst_i = [0]

        def flush(off, fn_fill, rkeys):
            i_ = wst_i[0] % 2
            wst_i[0] += 1
            fn_fill(wst[:, i_, :], ("wst", i_), rkeys)
            E.dma("sp", lambda e: e.dma_start(out=ssmw_d[l][:, off:off + 1024], in_=wst[:, i_, :]), r=[("wst", i_)], w=[("ssmw_d", l)], chan="ldw")
        for which in range(2):
            for tile_ in range(2):
                for x2 in range(2):
                    b = bank()
                    for s in range(8):
                        if which == 0:
                            src = WbT2[:, x2, s, tile_ * 4:tile_ * 4 + 4, :].rearrange("p q c -> p (q c)")
                            rk = [("WbTl", l)]
                        else:
                            src = WaTs[:, tile_, x2, s, :]
                            rk = [("WaTsl", l)]
                        E.op("pe", lambda e: e.transpose(out=PSB[:, b, s * 128:(s + 1) * 128], in_=src, identity=identB),
                             r=rk + ["identB"], w=[P(b)], sig=(s == 7))
                    off = which * 4096 + (tile_ * 2 + x2) * 1024
                    flush(off, lambda dst, k_, rk_: E.op("act", lambda e: e.activation(out=dst, in_=PSB[:, b, :], func=AF.Copy), r=rk_, w=[k_]), [P(b)])
        cplx(16, cre, cim, ck, gq_out(Gr, False, "Gr"), gq_out(nGi, True, "nGi"), True, ["Gr", "nGi"])
        flush(8192, lambda dst, k_, rk_: E.op("act", lambda e: e.activation(out=dst, in_=Gr.rearrange("p a b -> p (a b)"), func=AF.Copy), r=rk_, w=[k_]), ["Gr"])
        flush(9216, lambda dst, k_, rk_: E.op("act", lambda e: e.activation(out=dst, in_=nGi.rearrange("p a b -> p (a b)"), func=AF.Copy), r=rk_, w=[k_]), ["nGi"])

    m0 = A.mark()
    prep_layer(0)
    E.barrier()
    A.release(m0)
    E.barrier()
    def xkeys(bi):
        return [("xT", bi)]

    def rmsnorm(dst, dst_key, gcol_fn, gkey, scratch):
        sq2, rt2 = scratch
        for bi, (c0, n) in enumerate(BLKS):
            pb = bi % 2
            sq = sq2[:, pb]; rt = rt2[:, pb]
            for dt_ in range(DT):
                E.op("act", lambda e: e.activation(
                    out=sq[:, dt_, 0:n], in_=xT[:, dt_, c0:c0 + n], func=AF.Square), r=xkeys(bi), w=[("sq", pb, dt_)])
            b = bank()
            for dt_ in range(DT):
                E.op("pe", lambda e: e.matmul(
                    PS[:, b, 0:n], lhsT=onesB, rhs=sq[:, dt_, 0:n], start=(dt_ == 0), stop=(dt_ == DT - 1)),
                    r=[("sq", pb, dt_), "onesB"], w=[P(b)], sig=(dt_ == DT - 1))
            if USE_ARS:
                E.op("act", lambda e: e.activation(
                    out=rt[:, 0, 0:n], in_=PS[:, b, 0:n], func=AF.Ln, bias=cst[:, 0:1], scale=1.0 / D), r=[P(b), "cst"], w=[("rt0", pb)])
                E.op("act", lambda e: e.activation(
                    out=rt[:, 1, 0:n], in_=rt[:, 0, 0:n], func=AF.Exp, scale=-0.5), r=[("rt0", pb)], w=[("rt1", pb)])
            else:
                E.op("act", lambda e: e.activation(
                    out=rt[:, 0, 0:n], in_=PS[:, b, 0:n], func=AF.Sqrt, bias=cst[:, 0:1], scale=1.0 / D), r=[P(b), "cst"], w=[("rt0", pb)])
                E.op("dve", lambda e: e.reciprocal(out=rt[:, 1, 0:n], in_=rt[:, 0, 0:n]), r=[("rt0", pb)], w=[("rt1", pb)])
            for dt_ in range(DT):
                E.op("dve", lambda e: e.scalar_tensor_tensor(
                    out=dst[:, dt_, c0:c0 + n], in0=xT[:, dt_, c0:c0 + n], scalar=gcol_fn(dt_),
                    in1=rt[:, 1, 0:n], op0=ALU.mult, op1=ALU.mult), r=xkeys(bi) + [("rt1", pb), gkey], w=[(dst_key, bi)])

    def mm_group(bk, n, steps, rkeys, wkey=None):
        ns = len(steps)
        for i, st in enumerate(steps):
            lh, rh = st[0], st[1]
            tp = st[2] if len(st) > 2 else None
            if tp is None:
                fn = lambda e, lh=lh, rh=rh, i=i: e.matmul(PS[:, bk, 0:n], lhsT=lh, rhs=rh, start=(i == 0), stop=(i == ns - 1))
            else:
                fn = lambda e, lh=lh, rh=rh, i=i, tp=tp: e.matmul(PS[:, bk, 0:n], lhsT=lh, rhs=rh, start=(i == 0), stop=(i == ns - 1),
                                                                   tile_position=tp)
            E.op("pe", fn, r=rkeys, w=[P(bk)], sig=(i == ns - 1))

    pre = {}

    def prefetch_proj(l, col0):
        sl = wslot()
        wv = wdma(sl, 0, [8, 256], w_in[l][:, col0:col0 + 256].rearrange("(kt p) n -> p kt n", p=128))
        pre[("proj", l, col0)] = (sl, wv)

    def gate_w(l, j2, k):
        key = ("gate", l, j2, k)
        if key in pre:
            return pre.pop(key)
        sl = wslot()
        c0w = OFF_GATE + k * D + j2 * 256
        wg = wdma(sl, 0, [8, 256], w_in[l][:, c0w:c0w + 256].rearrange("(kt p) n -> p kt n", p=128))
        wb = wdma(sl, 2048, [2, 256], w_branch[l, k][:, j2 * 256:(j2 + 1) * 256].rearrange("(kt p) n -> p kt n", p=128))
        return (sl, wg, wb)

    def ffn_w(l, f):
        key = ("ffn", l, f)
        if key in pre:
            return pre.pop(key)
        sl = wslot()
        wg_ = wdma(sl, 0, [8, 128], w_ffn_in[l][:, f * 128:(f + 1) * 128].rearrange("(kt p) n -> p kt n", p=128))
        wu_ = wdma(sl, 1024, [8, 128], w_ffn_in[l][:, FF + f * 128:FF + (f + 1) * 128].rearrange("(kt p) n -> p kt n", p=128))
        return (sl, wg_, wu_)

    def proj_group(l, col0, evac):
        if ("proj", l, col0) not in pre:
            prefetch_proj(l, col0)
        sl, wv = pre.pop(("proj", l, col0))
        for ct in range(2):
            for bi, (c0, n) in enumerate(BLKS):
                bk = bank()
                mm_group(bk, n, [(wv[:, kt, ct * 128:(ct + 1) * 128], hB[:, kt, c0:c0 + n]) for kt in range(DT)],
                         [("ws", sl), ("h", bi)])
                evac(ct, bi, c0, n, bk)

    def ext_views(extp, exts, Hh, ct, bi, c0, n):
        if bi < 4:
            return extp[:, ct, Hh + c0:Hh + c0 + n]
        return exts[:, ct, Hh:Hh + LS, :].rearrange("p j b -> p b j")

    def load_hist(l, src_d, Hh, exts, key):
        rows = NS * Hh
        nb_t = max(1, 128 // Hh) if Hh > 8 else NS
        nb_t = min(nb_t, NS)
        while NS % nb_t:
            nb_t -= 1
        m = A.mark()
        hst = A.alloc([2, C], F32)
        for ti, b0 in enumerate(range(0, NS, nb_t)):
            nr = nb_t * Hh
            sb = ti % 2
            E.dma("sp", lambda e, b0=b0, nr=nr, sb=sb: e.dma_start(out=hst[0:nr, sb, :], in_=src_d[l, b0 * Hh:b0 * Hh + nr, :]),
                  w=[("hst", sb)], chan="lh%d" % sb)
            for ct in range(2):
                bk = bank()
                E.op("pe", lambda e, bk=bk, nr=nr, sb=sb, ct=ct: e.transpose(out=PS[:, bk, 0:nr], in_=hst[0:nr, sb, ct * 128:(ct + 1) * 128],
                                                                               identity=identF[0:nr, 0:nr]),
                     r=[("hst", sb), "identF"], w=[P(bk)])
                E.op("dve", lambda e, bk=bk, nr=nr, ct=ct, b0=b0: e.tensor_copy(
                    out=exts[:, ct, 0:Hh, b0:b0 + nb_t].rearrange("p r b -> p b r"), in_=PS[:, bk, 0:nr].rearrange("p (b r) -> p b r", r=Hh)),
                    r=[P(bk)], w=[key])

    def store_hist(l, Hh, extp, exts, key, out_p, out_s):
        m = A.mark()
        ost = A.alloc([8, C], F32)
        for ct in range(2):
            bk = bank()
            E.op("pe", lambda e: e.transpose(out=PSB[0:Hh, bk, 0:128], in_=extp[:, ct, TP:TP + Hh], identity=identB),
                 r=[key, "identB"], w=[P(bk)])
            E.op("act", lambda e: e.activation(out=ost[0:Hh, 0, ct * 128:(ct + 1) * 128], in_=PSB[0:Hh, bk, 0:128], func=AF.Copy),
                 r=[P(bk)], w=["ost"])
        E.dma("sp", lambda e: e.dma_start(out=out_p[l], in_=ost[0:Hh, 0, :]), r=["ost"], chan="so0")
        for b0 in (0, 8):
            for ct in range(2):
                bk = bank()
                for bb in range(8):
                    E.op("pe", lambda e: e.transpose(out=PSB[0:Hh, bk, bb * 128:(bb + 1) * 128], in_=exts[:, ct, LS:LS + Hh, b0 + bb], identity=identB),
                         r=[key, "identB"], w=[P(bk)], sig=(bb == 7))
                E.op("act", lambda e: e.activation(out=ost[0:Hh, :, ct * 128:(ct + 1) * 128],
                                                   in_=PSB[0:Hh, bk, :].rearrange("p (b c) -> p b c", c=128), func=AF.Copy),
                     r=[P(bk)], w=["ost"])
            dst = out_s[l, b0 * Hh:(b0 + 8) * Hh, :].rearrange("(b r) c -> r b c", r=Hh)
            E.dma("sp", lambda e: e.dma_start(out=dst, in_=ost[0:Hh, :, :]), r=["ost"], chan="so0")

    def diag_build(dst, wcol_fn, ntap, key):
        for k in range(ntap):
            E.op("act", lambda e, k=k: e.activation(out=dst[:, k, :], in_=identF, func=AF.Copy, scale=wcol_fn(k)),
                 r=["identF", "parT"], w=[key])

    def conv_taps(ntap, Hh, dg, extp, exts, ct, bi, c0, n):
        steps = []
        for k in range(ntap):
            sh = Hh - (ntap - 1) + k
            if bi < 4:
                rh = extp[:, ct, sh + c0:sh + c0 + n]
            else:
                rh = exts[:, ct, sh:sh + LS, :].rearrange("p j b -> p (j b)")
            steps.append((dg[:, ct, k, :], rh))
        return steps
    GELU_C = 1.5957691216057308
    for l in range(nlayers):
        m_l = A.mark()
        ybr = [None] * 4
        ybr[1] = A.alloc([2, T], BF16)
        m_s = A.mark()
        Wall = A.alloc([10240], BF16)
        E.dma("sp", lambda e, l=l: e.dma_start(out=Wall, in_=ssmw_d[l]), r=[("ssmw_d", l)], w=["Wall"], chan="ldw")
        m_n = A.mark()
        sq = A.alloc([2, DT, 512], BF16); rt = A.alloc([2, 2, 512], F32)
        rmsnorm(hB, "h", lambda dt_: par(l, dt_), "parT", (sq, rt))
        A.release(m_n)
        prefetch_proj(l, OFF_SSM)
        E.barrier()
        Wb_l = Wall[:, 0:4096].rearrange("p (t r s c) -> p t r s c", t=2, r=2, s=8)
        Wa_l = Wall[:, 4096:8192].rearrange("p (t r s c) -> p t r s c", t=2, r=2, s=8)
        Qr = Wall[:, 8192:9216].rearrange("p (q c) -> p q c", q=8)
        nQi = Wall[:, 9216:10240].rearrange("p (q c) -> p q c", q=8)
        u_fm = A.alloc([2, 8, NCH], BF16)
        z_fm = A.alloc([2, T], BF16)
        h0 = A.alloc([2, 8, NS], F32)
        fin_s = A.alloc([2, 8, NS], F32)
        fin_p = A.alloc([2, 8], F32)
        Ych = A.alloc([8, NCH], BF16)
        hs_st1 = A.alloc([1024], F32)
        hs_st = fv(hs_st1, [[0, 2], [1, 1024]])
        pmat = A.alloc([2, 8, 128], BF16)
        ld(pmat, pmat_d, ["pmat"], chan="ldp", q="pool")
        for ri, src in enumerate((st_re, st_im)):
            E.dma("sp", lambda e, ri=ri, src=src: e.dma_start(out=hs_st[0:NS, ri, :], in_=src[l]), w=["hs_st"], chan="lh%d" % ri)
            bk = bank()
            for q in range(8):
                E.op("pe", lambda e, bk=bk, q=q, ri=ri: e.transpose(out=PS[:, bk, q * NS:(q + 1) * NS], in_=hs_st[0:NS, ri, q * 128:(q + 1) * 128],
                                                                     identity=identF[0:NS, 0:NS]), r=["hs_st", "identF"], w=[P(bk)], sig=(q == 7))
            E.op("dve", lambda e, bk=bk, ri=ri: e.tensor_copy(out=h0[:, ri].rearrange("p q b -> p (q b)"), in_=PS[:, bk, 0:128]),
                 r=[P(bk)], w=["h0"])

        def ev_u(ct, bi, c0, n, bk):
            E.op("act", lambda e: e.activation(out=u_fm[:, ct, :, c0 // 8:(c0 + n) // 8].rearrange("p s c -> p c s"),
                                               in_=PS[:, bk, 0:n].rearrange("p (c s) -> p c s", s=8), func=AF.Copy), r=[P(bk)], w=[("u_fm", ct)])
        proj_group(l, OFF_SSM, ev_u)

        def usub(tile_, q4, s):
            return u_fm[32 * q4:32 * q4 + 32, tile_, s, :]

        sbuf_ = A.alloc([2, 2, 2, 384], F32)
        E.op("dve", lambda e: e.memset(sbuf_.rearrange("p a b c d -> p (a b c d)"), 0.0), w=[("sb", a_, b_, c_) for a_ in range(2) for b_ in range(2) for c_ in range(2)])
        Ssm = A.alloc([2, 2, NS], F32)
        HpB = A.alloc([2, 2, NCH], BF16)
        Y2s = A.alloc([2, NCH], F32)

        def emit_Y2(q, qp):
            tile_, q4 = q // 4, q % 4
            for e_ in range(2):
                g = 2 * q + e_
                bk1 = Y1banks[(q, e_)]
                bk2 = bank()
                mm_group(bk2, NCH, [(Qr[64 * e_:64 * e_ + 64, q, :], HpB[64 * e_:64 * e_ + 64, qp, 0, :], (64 * e_, 0)),
                                    (nQi[64 * e_:64 * e_ + 64, q, :], HpB[64 * e_:64 * e_ + 64, qp, 1, :], (64 * e_, 0))],
                         ["Wall", ("HpB", qp)])
                E.op("act", lambda e, bk2=bk2, e_=e_: e.activation(out=Y2s[:, e_, :], in_=PS[:, bk2, 0:NCH], func=AF.Copy),
                     r=[P(bk2)], w=[("Y2s", e_)])
                E.op("dve", lambda e, bk1=bk1, e_=e_, g=g: e.tensor_tensor(out=Ych[:, g % 8, :], in0=PS[:, bk1, 0:NCH], in1=Y2s[:, e_, :], op=ALU.add),
                     r=[P(bk1), ("Y2s", e_)], w=[("Ych", g % 8)])
                held.discard(bk1)

        def placement(tile_):
            for j in range(8):
                bk = bank()
                rb = 32 * (j // 2)
                mm_group(bk, NCH, [(pmat[rb:rb + 32, j % 2, g8, :], Ych[rb:rb + 32, g8, :], (rb, 0)) for g8 in range(8)],
                         ["pmat"] + [("Ych", g8) for g8 in range(8)])
                E.op("act", lambda e, bk=bk: e.activation(out=gl[:, 0, :], in_=PS[:, bk, 0:NCH], func=AF.Square, scale=0.044715 ** 0.5), r=[P(bk)], w=["gl0"])
                E.op("dve", lambda e, bk=bk: e.scalar_tensor_tensor(out=gl[:, 1, :], in0=gl[:, 0, :], scalar=1.0, in1=PS[:, bk, 0:NCH], op0=ALU.add, op1=ALU.mult),
                     r=["gl0", P(bk)], w=["gl1"])
                E.op("act", lambda e: e.activation(out=gl[:, 2, :], in_=gl[:, 1, :], func=AF.Sigmoid, scale=GELU_C), r=["gl1"], w=["gl2"])
                E.op("dve", lambda e, bk=bk, j=j: e.tensor_tensor(out=fv(z_fm[:, tile_, :], [[8, NCH]], off=j), in0=gl[:, 2, :], in1=PS[:, bk, 0:NCH], op=ALU.mult),
                     r=["gl2", P(bk)], w=[("z_fm", tile_)])

        gl = A.alloc([3, NCH], F32)
        Y1banks = {}

        def emit_S_Y1(q):
            tile_, q4, qp = q // 4, q % 4, q % 2
            rows = slice(32 * q4, 32 * q4 + 32)
            tp = (32 * q4, 0)
            for ri in range(2):
                bk = bank()
                mm_group(bk, NCH, [(Wb_l[rows, tile_, ri, s, :], usub(tile_, q4, s), tp) for s in range(8)], ["Wall", ("u_fm", tile_)])
                E.op("act", lambda e: e.activation(out=sbuf_[:, qp, 0, ri, 128:384], in_=PS[:, bk, 0:256], func=AF.Copy),
                     r=[P(bk)], w=[("sb", qp, 0, ri)])
                E.op("act", lambda e: e.activation(out=Ssm[:, qp, ri, :], in_=PS[:, bk, 256:NCH], func=AF.Copy),
                     r=[P(bk)], w=[("Ssm", qp)])
            for e_ in range(2):
                bk = bank()
                Y1banks[(q, e_)] = bk
                held.add(bk)
                mm_group(bk, NCH, [(Wa_l[rows, tile_, e_, s, :], usub(tile_, q4, s), tp) for s in range(8)], ["Wall", ("u_fm", tile_)])

        def scan_level(q, lev, cur):
            qp = q % 2
            d = 1 << lev
            nxt = 1 - cur

            def cc(c):
                return fv(scoef[:, l], [[1, 1]], off=q * 24 + lev * 3 + c)
            E.op("dve", lambda e: e.scalar_tensor_tensor(
                out=sbuf_[:, qp, nxt, :, 128:384], in0=sbuf_[:, qp, cur, :, 128 - d:384 - d], scalar=cc(0), in1=sbuf_[:, qp, cur, :, 128:384],
                op0=ALU.mult, op1=ALU.add), r=[("sb", qp, cur, 0), ("sb", qp, cur, 1), "scoef"], w=[("sb", qp, nxt, 0), ("sb", qp, nxt, 1)])
            for ri in range(2):
                a_oth = sbuf_[:, qp, cur, 1 - ri, 128 - d:384 - d]
                coth = cc(2) if ri == 0 else cc(1)
                E.op("dve", lambda e: e.scalar_tensor_tensor(
                    out=sbuf_[:, qp, nxt, ri, 128:384], in0=a_oth, scalar=coth, in1=sbuf_[:, qp, nxt, ri, 128:384],
                    op0=ALU.mult, op1=ALU.add), r=[("sb", qp, cur, 0), ("sb", qp, cur, 1), ("sb", qp, nxt, ri), "scoef"], w=[("sb", qp, nxt, ri)])

        def post_scan(q):
            qp = q % 2
            cur = 0
            for ri in range(2):
                E.op("act", lambda e: e.activation(out=HpB[:, qp, ri, 0:256], in_=sbuf_[:, qp, cur, ri, 127:383], func=AF.Copy),
                     r=[("sb", qp, 0, ri)], w=[("HpB", qp)])
                E.op("act", lambda e: e.activation(out=HpB[:, qp, ri, 256:NCH], in_=h0[:, ri, q, :], func=AF.Copy),
                     r=["h0"], w=[("HpB", qp)])
                E.op("act", lambda e: e.activation(out=fin_p[:, ri, q:q + 1], in_=sbuf_[:, qp, cur, ri, 383:384], func=AF.Copy),
                     r=[("sb", qp, 0, ri)], w=["fin_p"])
            c0_ = fv(scoef[:, l], [[1, 1]], off=q * 24 + 0); c1_ = fv(scoef[:, l], [[1, 1]], off=q * 24 + 1); c2_ = fv(scoef[:, l], [[1, 1]], off=q * 24 + 2)
            for ri in range(2):
                E.op("dve", lambda e: e.scalar_tensor_tensor(out=fin_s[:, ri, q, :], in0=h0[:, ri, q, :], scalar=c0_, in1=Ssm[:, qp, ri, :],
                                                             op0=ALU.mult, op1=ALU.add), r=["h0", ("Ssm", qp), "scoef"], w=["fin_s"])
                E.op("dve", lambda e: e.scalar_tensor_tensor(out=fin_s[:, ri, q, :], in0=h0[:, 1 - ri, q, :], scalar=(c2_ if ri == 0 else c1_),
                                                             in1=fin_s[:, ri, q, :], op0=ALU.mult, op1=ALU.add),
                     r=["h0", "fin_s", "scoef"], w=["fin_s"])

        for q0 in (0, 2, 4, 6):
            for q in (q0, q0 + 1):
                emit_S_Y1(q)
            cur = 0
            for lev in range(8):
                for q in (q0, q0 + 1):
                    scan_level(q, lev, cur)
                cur = 1 - cur
            for q in (q0, q0 + 1):
                post_scan(q)
            for q in (q0, q0 + 1):
                emit_Y2(q, q % 2)
            if q0 == 2:
                placement(0)
            if q0 == 6:
                placement(1)
        for ri, (op_, os_) in enumerate(((o_re_p, o_re_s), (o_im_p, o_im_s))):
            bk = bank()
            E.op("pe", lambda e, bk=bk, ri=ri: e.transpose(out=PS[0:8, bk, 0:128], in_=fin_p[:, ri, :], identity=identF), r=["fin_p", "identF"], w=[P(bk)])
            E.op("act", lambda e, bk=bk, ri=ri: e.activation(out=hs_st[0:8, ri, 0:128], in_=PS[0:8, bk, 0:128], func=AF.Copy), r=[P(bk)], w=["hs_st"])
            E.dma("sp", lambda e, ri=ri, op_=op_: e.dma_start(out=op_[l], in_=hs_st[0:8, ri, 0:128]), r=["hs_st"], chan="so%d" % ri)
            bks = [bank(), bank()]
            for q in range(8):
                E.op("pe", lambda e, q=q, ri=ri, bks=bks: e.transpose(out=PS[0:NS, bks[q // 4], (q % 4) * 128:(q % 4 + 1) * 128], in_=fin_s[:, ri, q, :], identity=identF),
                     r=["fin_s", "identF"], w=[P(bks[q // 4])], sig=(q % 4 == 3))
            for hh in range(2):
                E.op("act", lambda e, ri=ri, hh=hh, bks=bks: e.activation(out=hs_st[0:NS, ri, hh * 512:(hh + 1) * 512], in_=PS[0:NS, bks[hh], :], func=AF.Copy),
                     r=[P(bks[hh])], w=["hs_st"])
            E.dma("sp", lambda e, ri=ri, os_=os_: e.dma_start(out=os_[l], in_=hs_st[0:NS, ri, :]), r=["hs_st"], chan="so%d" % ri)
        if dbg == "zfm" and l == 0:
            for ct in range(2):
                for wi, (src_, kk) in enumerate(((z_fm, "z_fm"),)):
                    for hf in range(4):
                        m_d = A.mark()
                        dtmp = A.alloc([544], F32)
                        E.op("dve", lambda e: e.tensor_copy(out=dtmp, in_=src_[:, ct, hf * 544:(hf + 1) * 544]), r=[(kk, ct)], w=["dtmp"])
                        E.dma("sp", lambda e: e.dma_start(out=dbg_out[wi, ct][:, hf * 544:(hf + 1) * 544], in_=dtmp), r=["dtmp"], chan="dbg")
                        E.barrier()
                        A.release(m_d)
        sl = wslot()
        wg = wdma(sl, 0, [2, 256], w_glu[l].rearrange("(kt p) n -> p kt n", p=128))
        sgt = fv(sbuf_, [[512, 2], [1, 512]])
        for ct in range(2):
            for bi, (c0, n) in enumerate(BLKS):
                bk = bank()
                mm_group(bk, n, [(wg[:, kt, ct * 128:(ct + 1) * 128], z_fm[:, kt, c0:c0 + n]) for kt in range(2)], [("ws", sl), ("z_fm", 0), ("z_fm", 1)])
                sb = bi % 2
                E.op("act", lambda e, bk=bk, n=n, sb=sb, ct=ct: e.activation(out=sgt[:, sb, 0:n], in_=PS[:, bk, 0:n], func=AF.Sigmoid, bias=par(l, 50 + ct)),
                     r=[P(bk), "parT"] + [("sb", 0, b_, c_) for b_ in range(2) for c_ in range(2)], w=[("sb", 0, b_, c_) for b_ in range(2) for c_ in range(2)] + [("sgt", sb)])
                E.op("dve", lambda e, n=n, sb=sb, ct=ct, c0=c0: e.tensor_tensor(out=ybr[1][:, ct, c0:c0 + n], in0=sgt[:, sb, 0:n], in1=z_fm[:, ct, c0:c0 + n], op=ALU.mult),
                     r=[("sgt", sb), ("z_fm", ct)], w=[("ybr", 1)])
        prefetch_proj(l, OFF_CF + 256)
        E.barrier()
        A.release(m_s)
        ybr[3] = A.alloc([2, T], BF16)
        m_c = A.mark()
        Hh = 30
        extp = A.alloc([2, Hh + TP], BF16); exts = A.alloc([2, Hh + LS, NS], BF16)
        sigb = A.alloc([2, T], BF16)
        dg = A.alloc([2, 31, 128], BF16)
        E.op("dve", lambda e: e.memset(extp[:, :, 0:Hh], 0.0), w=["cf_ext"])
        load_hist(l, st_cf, Hh, exts, "cf_ext")
        for ct in range(2):
            diag_build(dg[:, ct], lambda k, ct=ct: par(l, 64 + 2 * k + ct), 31, "cf_dg")

        def ev_b(ct, bi, c0, n, bk):
            E.op("act", lambda e: e.activation(out=sigb[:, ct, c0:c0 + n], in_=PS[:, bk, 0:n], func=AF.Sigmoid), r=[P(bk)], w=[("sigb", ct)])
        proj_group(l, OFF_CF + 256, ev_b)

        def ev_a(ct, bi, c0, n, bk):
            dst = ext_views(extp, exts, Hh, ct, bi, c0, n)
            i1 = sigb[:, ct, c0:c0 + n] if bi < 4 else sigb[:, ct, c0:c0 + n].rearrange("p (b j) -> p b j", j=LS)
            i0 = PS[:, bk, 0:n] if bi < 4 else PS[:, bk, 0:n].rearrange("p (b j) -> p b j", j=LS)
            E.op("dve", lambda e: e.tensor_tensor(out=dst, in0=i0, in1=i1, op=ALU.mult), r=[P(bk), ("sigb", ct)], w=["cf_ext"])
        proj_group(l, OFF_CF, ev_a)
        store_hist(l, Hh, extp, exts, "cf_ext", o_cf_p, o_cf_s)
        ycv = A.alloc([2, 2, 512], BF16)
        mu = A.alloc([512], F32); var = A.alloc([512], F32); t1 = A.alloc([512], F32)
        for bi, (c0, n) in enumerate(BLKS):
            for ct in range(2):
                bk = bank()
                mm_group(bk, n, conv_taps(31, Hh, dg, extp, exts, ct, bi, c0, n), ["cf_dg", "cf_ext"])
                E.op("act", lambda e: e.activation(out=ycv[:, ct, 0, 0:n], in_=PS[:, bk, 0:n], func=AF.Copy), r=[P(bk)], w=[("ycv", ct, 0)])
                E.op("act", lambda e: e.activation(out=ycv[:, ct, 1, 0:n], in_=PS[:, bk, 0:n], func=AF.Square), r=[P(bk)], w=[("ycv", ct, 1)])
            b1 = bank(); b2 = bank()
            mm_group(b1, n, [(onesB, ycv[:, ct, 0, 0:n]) for ct in range(2)], ["onesB", ("ycv", 0, 0), ("ycv", 1, 0)])
            mm_group(b2, n, [(onesB, ycv[:, ct, 1, 0:n]) for ct in range(2)], ["onesB", ("ycv", 0, 1), ("ycv", 1, 1)])
            E.op("act", lambda e: e.activation(out=mu[:, 0:n], in_=PS[:, b1, 0:n], func=AF.Copy, scale=1.0 / C), r=[P(b1)], w=["mu"])
            E.op("dve", lambda e: e.tensor_tensor(out=t1[:, 0:n], in0=mu[:, 0:n], in1=mu[:, 0:n], op=ALU.mult), r=["mu"], w=["t1"])
            E.op("dve", lambda e: e.scalar_tensor_tensor(out=var[:, 0:n], in0=PS[:, b2, 0:n], scalar=1.0 / C, in1=t1[:, 0:n], op0=ALU.mult, op1=ALU.subtract),
                 r=[P(b2), "t1"], w=["var"])
            E.op("act", lambda e: e.activation(out=var[:, 0:n], in_=var[:, 0:n], func=AF.Sqrt, bias=cst[:, 1:2]), r=["var", "cst"], w=["var"])
            E.op("dve", lambda e: e.reciprocal(out=var[:, 0:n], in_=var[:, 0:n]), r=["var"], w=["var"])
            for ct in range(2):
                E.op("dve", lambda e: e.tensor_tensor(out=t1[:, 0:n], in0=ycv[:, ct, 0, 0:n], in1=mu[:, 0:n], op=ALU.subtract), r=[("ycv", ct, 0), "mu"], w=["t1"])
                E.op("dve", lambda e: e.tensor_tensor(out=t1[:, 0:n], in0=t1[:, 0:n], in1=var[:, 0:n], op=ALU.mult), r=["t1", "var"], w=["t1"])
                o_ = ybr[3][:, ct, c0:c0 + n] if bi < 4 else ybr[3][:, ct, c0:c0 + n].rearrange("p (b j) -> p j b", j=LS)
                i_ = t1[:, 0:n] if bi < 4 else t1[:, 0:n].rearrange("p (j b) -> p j b", b=NS)
                E.op("act", lambda e: e.activation(out=o_, in_=i_, func=AF.Silu, bias=par(l, 54 + ct), scale=par(l, 52 + ct)),
                     r=["t1", "parT"], w=[("ybr", 3)])
        prefetch_proj(l, OFF_SC)
        E.barrier()
        A.release(m_c)
        ybr[2] = A.alloc([2, T], BF16)
        m_c = A.mark()
        Hh = 2
        extp = A.alloc([2, Hh + TP], BF16); exts = A.alloc([2, Hh + LS, NS], BF16)
        bgs = A.alloc([2, T], BF16); cgs = A.alloc([2, T], BF16)
        dg = A.alloc([2, 3, 128], BF16)
        E.op("dve", lambda e: e.memset(extp[:, :, 0:Hh], 0.0), w=["sc_ext"])
        load_hist(l, st_sc, Hh, exts, "sc_ext")
        for ct in range(2):
            diag_build(dg[:, ct], lambda k, ct=ct: par(l, 58 + 2 * k + ct), 3, "sc_dg")

        def ev_copy(dstbuf, key):
            def f(ct, bi, c0, n, bk):
                E.op("act", lambda e: e.activation(out=dstbuf[:, ct, c0:c0 + n], in_=PS[:, bk, 0:n], func=AF.Copy), r=[P(bk)], w=[(key, ct)])
            return f
        proj_group(l, OFF_SC, ev_copy(bgs, "bgs"))
        proj_group(l, OFF_SC + 256, ev_copy(cgs, "cgs"))

        def ev_hx(ct, bi, c0, n, bk):
            dst = ext_views(extp, exts, Hh, ct, bi, c0, n)
            i1 = cgs[:, ct, c0:c0 + n] if bi < 4 else cgs[:, ct, c0:c0 + n].rearrange("p (b j) -> p b j", j=LS)
            i0 = PS[:, bk, 0:n] if bi < 4 else PS[:, bk, 0:n].rearrange("p (b j) -> p b j", j=LS)
            E.op("dve", lambda e: e.tensor_tensor(out=dst, in0=i0, in1=i1, op=ALU.mult), r=[P(bk), ("cgs", ct)], w=["sc_ext"])
        proj_group(l, OFF_SC + 512, ev_hx)
        store_hist(l, Hh, extp, exts, "sc_ext", o_sc_p, o_sc_s)
        for bi, (c0, n) in enumerate(BLKS):
            for ct in range(2):
                bk = bank()
                mm_group(bk, n, conv_taps(3, Hh, dg, extp, exts, ct, bi, c0, n), ["sc_dg", "sc_ext"])
                o_ = ybr[2][:, ct, c0:c0 + n] if bi < 4 else ybr[2][:, ct, c0:c0 + n].rearrange("p (b j) -> p j b", j=LS)
                i0_ = PS[:, bk, 0:n] if bi < 4 else PS[:, bk, 0:n].rearrange("p (j b) -> p j b", b=NS)
                i1_ = bgs[:, ct, c0:c0 + n] if bi < 4 else bgs[:, ct, c0:c0 + n].rearrange("p (b j) -> p j b", j=LS)
                E.op("dve", lambda e: e.tensor_tensor(out=o_, in0=i0_, in1=i1_, op=ALU.mult),
                     r=[P(bk), ("bgs", ct)], w=[("ybr", 2)])
        prefetch_proj(l, 0)
        E.barrier()
        A.release(m_c)
        ybr[0] = A.alloc([2, T], BF16)
        m_c = A.mark()
        Hh = 15
        extp = A.alloc([2, Hh + TP], BF16); exts = A.alloc([2, Hh + LS, NS], BF16)
        wbd = A.alloc([2, 128], F32)
        wbdB = A.alloc([2, 128], BF16)
        pl = A.alloc([2, 16, 128], BF16)
        E.op("dve", lambda e: e.memset(extp[:, :, 0:Hh], 0.0), w=["pl_ext"])
        E.op("dve", lambda e: e.memset(wbd.rearrange("p a b -> p (a b)"), 0.0), w=["wbd"])
        for k in range(4):
            t_, hf = k // 2, k % 2
            ld(wbd[64 * hf:64 * hf + 64, t_, 64 * hf:64 * hf + 64], pool_w[l, k], ["wbd"], chan="lh0", r=["wbd"])
        load_hist(l, st_pool, Hh, exts, "pl_ext")
        ntaps = (4, 16)
        for t_ in range(2):
            E.op("act", lambda e: e.activation(out=wbdB[:, t_, :], in_=wbd[:, t_, :], func=AF.Copy), r=["wbd"], w=["wbdB"])
            for i in range(ntaps[t_]):
                E.op("act", lambda e: e.activation(out=pl[:, t_, i, :], in_=wbd[:, t_, :], func=AF.Copy, scale=poolc[:, t_, i:i + 1]),
                     r=["wbd", "poolc"], w=["pl"])

        def ev_pool(ct, bi, c0, n, bk):
            dst = ext_views(extp, exts, Hh, ct, bi, c0, n)
            i0 = PS[:, bk, 0:n] if bi < 4 else PS[:, bk, 0:n].rearrange("p (b j) -> p b j", j=LS)
            E.op("act", lambda e: e.activation(out=dst, in_=i0, func=AF.Copy), r=[P(bk)], w=["pl_ext"])
        proj_group(l, 0, ev_pool)
        store_hist(l, Hh, extp, exts, "pl_ext", o_pool_p, o_pool_s)

        def pool_steps(ct, bi, c0, n):
            steps = []
            for i in range(ntaps[ct]):
                sh = Hh - i
                rh = extp[:, ct, sh + c0:sh + c0 + n] if bi < 4 else exts[:, ct, sh:sh + LS, :].rearrange("p j b -> p (j b)")
                steps.append((pl[:, ct, i, :], rh))
            return steps
        pt = A.alloc([3, 16], F32)
        for ct in range(2):
            for bi, (c0, n) in enumerate(BLKS):
                bk = bank()
                mm_group(bk, n, pool_steps(ct, bi, c0, n), ["pl", "pl_ext"])
                o_ = ybr[0][:, ct, c0:c0 + n] if bi < 4 else ybr[0][:, ct, c0:c0 + n].rearrange("p (b j) -> p j b", j=LS)
                i_ = PS[:, bk, 0:n] if bi < 4 else PS[:, bk, 0:n].rearrange("p (j b) -> p j b", b=NS)
                E.op("act", lambda e: e.activation(out=o_, in_=i_, func=AF.Copy, scale=par(l, 48 + ct)),
                     r=[P(bk), "parT"], w=[("ybr", 0)])
            bka = bank(); bkx = bank()
            mm_group(bka, 16, pool_steps(ct, 0, 0, 16), ["pl", "pl_ext"])
            mm_group(bkx, 16, [(wbdB[:, ct, :], extp[:, ct, Hh:Hh + 16])], ["wbdB", "pl_ext"])
            E.op("dve", lambda e: e.tensor_tensor(out=pt[:, 0, :], in0=PS[:, bkx, 0:16], in1=poolrm1[:, ct, :], op=ALU.mult), r=[P(bkx), "poolc"], w=["pt0"])
            E.op("dve", lambda e: e.tensor_tensor(out=pt[:, 1, :], in0=PS[:, bka, 0:16], in1=poolr[:, ct, :], op=ALU.mult), r=[P(bka), "poolc"], w=["pt1"])
            E.op("dve", lambda e: e.tensor_tensor(out=pt[:, 2, :], in0=pt[:, 0, :], in1=pt[:, 1, :], op=ALU.add), r=["pt0", "pt1"], w=["pt2"])
            E.op("dve", lambda e: e.tensor_scalar(out=ybr[0][:, ct, 0:16], in0=pt[:, 2, :], scalar1=par(l, 48 + ct), scalar2=None, op0=ALU.mult),
                 r=["pt2", "parT", ("ybr", 0)], w=[("ybr", 0)])
        pre[("gate", l, 0, 0)] = gate_w(l, 0, 0)
        E.barrier()
        A.release(m_c)
        if dbg == "ybr" and l == 0:
            for k in range(4):
                for ct in range(2):
                    m_d = A.mark()
                    dtmp = A.alloc([T], F32)
                    E.op("dve", lambda e: e.tensor_copy(out=dtmp, in_=ybr[k][:, ct, :]), r=[("ybr", k)], w=["dtmp"])
                    E.dma("sp", lambda e: e.dma_start(out=dbg_out[k, ct], in_=dtmp), r=["dtmp"], chan="dbg")
                    E.barrier()
                    A.release(m_d)
        merged = A.alloc([DT, T], BF16)
        m_g = A.mark()
        sgt = A.alloc([2, 512], F32)
        tmpm = A.alloc([2, 512], F32)
        sg_i = 0
        for j2 in range(4):
            for k in range(4):
                sl, wg, wb = gate_w(l, j2, k)
                for jj in range(2):
                    j = 2 * j2 + jj
                    for bi, (c0, n) in enumerate(BLKS):
                        bg_ = bank()
                        mm_group(bg_, n, [(wg[:, kt, jj * 128:(jj + 1) * 128], hB[:, kt, c0:c0 + n]) for kt in range(DT)], [("ws", sl), ("h", bi)])
                        bb_ = bank()
                        mm_group(bb_, n, [(wb[:, kt, jj * 128:(jj + 1) * 128], ybr[k][:, kt, c0:c0 + n]) for kt in range(2)], [("ws", sl), ("ybr", k)])
                        si = sg_i % 2; sg_i += 1
                        E.op("act", lambda e: e.activation(out=sgt[:, si, 0:n], in_=PS[:, bg_, 0:n], func=AF.Sigmoid, bias=par(l, 16 + k * 8 + j)),
                             r=[P(bg_), "parT"], w=[("sgt", si)])
                        mk = ("merged", j, bi)
                        if k == 0:
                            E.op("dve", lambda e: e.tensor_tensor(out=merged[:, j, c0:c0 + n], in0=sgt[:, si, 0:n], in1=PS[:, bb_, 0:n], op=ALU.mult),
                                 r=[("sgt", si), P(bb_)], w=[mk])
                        else:
                            ti = sg_i % 2
                            E.op("dve", lambda e: e.tensor_tensor(out=tmpm[:, ti, 0:n], in0=sgt[:, si, 0:n], in1=PS[:, bb_, 0:n], op=ALU.mult),
                                 r=[("sgt", si), P(bb_)], w=[("tmpm", ti)])
                            E.op("dve", lambda e: e.tensor_tensor(out=merged[:, j, c0:c0 + n], in0=merged[:, j, c0:c0 + n], in1=tmpm[:, ti, 0:n], op=ALU.add),
                                 r=[("tmpm", ti), mk], w=[mk])
        for j2 in range(4):
            sl = wslot()
            wo = wdma(sl, 0, [8, 256], w_out[l][:, j2 * 256:(j2 + 1) * 256].rearrange("(kt p) n -> p kt n", p=128))
            for jj in range(2):
                j = 2 * j2 + jj
                for bi, (c0, n) in enumerate(BLKS):
                    bk = bank()
                    mm_group(bk, n, [(wo[:, kt, jj * 128:(jj + 1) * 128], merged[:, kt, c0:c0 + n]) for kt in range(DT)],
                             [("ws", sl)] + [("merged", kt, bi) for kt in range(DT)])
                    E.op("dve", lambda e: e.tensor_tensor(out=xT[:, j, c0:c0 + n], in0=xT[:, j, c0:c0 + n], in1=PS[:, bk, 0:n], op=ALU.add),
                         r=[P(bk), ("xT", bi)], w=[("xT2", bi, j)])
        E.barrier()
        A.release(m_l)
        sq = A.alloc([2, DT, 512], BF16); rt = A.alloc([2, 2, 512], F32)
        rmsnorm(hB, "h", lambda dt_: par(l, 8 + dt_), "parT", (sq, rt))
        A.release(m_l)
        pre[("ffn", l, 0)] = ffn_w(l, 0)
        E.barrier()
        coop = None
        GS = 11
        if l + 1 < nlayers:
            GS = 6
        ub = A.alloc([GS, T], BF16)
        sgt = A.alloc([3, 512], F32)
        sg_i = 0
        if l + 1 < nlayers:
            coop = Coop(lambda: prep_layer(l + 1))
            E.coop = coop
        groups = [(f0, min(f0 + GS, NFT)) for f0 in range(0, NFT, GS)]
        for (f0, f1) in groups:
            for fi, f in enumerate(range(f0, f1)):
                sl, wg_, wu_ = ffn_w(l, f)
                for bi, (c0, n) in enumerate(BLKS):
                    bg_ = bank()
                    mm_group(bg_, n, [(wg_[:, kt, :], hB[:, kt, c0:c0 + n]) for kt in range(DT)], [("ws", sl), ("h", bi)])
                    bu_ = bank()
                    mm_group(bu_, n, [(wu_[:, kt, :], hB[:, kt, c0:c0 + n]) for kt in range(DT)], [("ws", sl), ("h", bi)])
                    si = sg_i % 3; sg_i += 1
                    E.op("act", lambda e: e.activation(out=sgt[:, si, 0:n], in_=PS[:, bg_, 0:n], func=AF.Silu), r=[P(bg_)], w=[("sgt", si)])
                    E.op("dve", lambda e: e.tensor_tensor(out=ub[:, fi, c0:c0 + n], in0=sgt[:, si, 0:n], in1=PS[:, bu_, 0:n], op=ALU.mult),
                         r=[("sgt", si), P(bu_)], w=[("ub", fi, bi)])
                    if coop is not None:
                        coop.step(2)
            nf = f1 - f0
            for j in range(DT):
                sl = wslot()
                wo = wdma(sl, 0, [nf, 128], w_ffn_out[l][f0 * 128:f1 * 128, j * 128:(j + 1) * 128].rearrange("(kt p) n -> p kt n", p=128))
                for bi, (c0, n) in enumerate(BLKS):
                    bk = bank()
                    mm_group(bk, n, [(wo[:, kt, :], ub[:, kt, c0:c0 + n]) for kt in range(nf)], [("ws", sl)] + [("ub", kt, bi) for kt in range(nf)])
                    E.op("dve", lambda e: e.tensor_tensor(out=xT[:, j, c0:c0 + n], in0=xT[:, j, c0:c0 + n], in1=PS[:, bk, 0:n], op=ALU.add),
                         r=[P(bk), ("xT", bi), ("xT2", bi, j)], w=[("xT2", bi, j)])
                    if coop is not None:
                        coop.step(1)
        if coop is not None:
            coop.finish()
            E.coop = None
        E.barrier()
        for bi in range(5):
            E.lastw[("xT", bi)] = None
        A.release(m_l)

    m0 = A.mark()
    yT = hB
    yTf = xT
    sq = A.alloc([2, DT, 512], BF16)
    rt = A.alloc([2, 2, 512], F32)
    ost = A.alloc([2, D], F32)
    rmsnorm(yTf, "yT", lambda dt_: gfin[:, dt_:dt_ + 1], "gfin", (sq, rt))
    for i in range(T // 128):
        sb = i % 2
        bi = min(i // 4, 4)
        for half in range(2):
            b = bank()
            for q in range(4):
                dt_ = half * 4 + q
                E.op("pe", lambda e: e.transpose(out=PS[:, b, q * 128:(q + 1) * 128], in_=yTf[:, dt_, i * 128:(i + 1) * 128], identity=identF),
                     r=[("yT", bi), "identF"], w=[P(b)], sig=(q == 3))
            if half:
                E.op("act", lambda e: e.activation(out=ost[:, sb, 512:1024], in_=PS[:, b, :], func=AF.Copy), r=[P(b)], w=[("ost", sb, 1)])
            else:
                E.op("dve", lambda e: e.tensor_copy(out=ost[:, sb, 0:512], in_=PS[:, b, :]), r=[P(b)], w=[("ost", sb, 0)])
        E.dma("sp", lambda e: e.dma_start(out=y_out[i * 128:(i + 1) * 128, :], in_=ost[:, sb, :]),
              r=[("ost", sb, 0), ("ost", sb, 1)], chan="st%d" % sb)
    A.release(m0)
    E.final_wait("sp")
    E.emit()
    print("ops", E.nops, "arena peak words", A.peak)
    return nc


_NC_CACHE = {}
_W_NAMES = ["norm1_g", "norm2_g", "final_g", "w_in", "b_gate", "pool_w", "pool_scale", "lam_re", "lam_im", "log_dt",
            "b_re", "b_im", "c_re", "c_im", "d_skip", "w_glu", "b_glu", "sc_w", "cf_w", "cf_ln_g", "cf_ln_b",
            "w_branch", "w_out", "w_ffn_in", "w_ffn_out"]


def kernel(**inputs):
    f = lambda a: np.ascontiguousarray(np.asarray(a, dtype=np.float32))
    x_prompt = f(inputs["x_prompt"]); x_sample = f(inputs["x_sample"])
    if "nc" not in _NC_CACHE:
        _NC_CACHE["nc"] = build()
    nc = _NC_CACHE["nc"]
    consts = host_consts()
    wts = {k: f(inputs[k]) for k in _W_NAMES}
    sp = f(inputs["state_pool"]); sr = f(inputs["state_ssm_re"]); si = f(inputs["state_ssm_im"])
    ssc = f(inputs["state_shortconv"]); scf = f(inputs["state_conformer"])
    in_maps = []
    for c in range(NCORES):
        b0, b1 = c * NS, (c + 1) * NS
        m = dict(wts)
        m.update(consts)
        m["xin"] = np.ascontiguousarray(np.concatenate([x_prompt[c], x_sample[b0:b1].reshape(NS * LS, D)], axis=0))
        m["st_pool"] = np.ascontiguousarray(sp[:, b0:b1].reshape(DEPTH, NS * 15, C))
        m["st_re"] = np.ascontiguousarray(sr[:, b0:b1].reshape(DEPTH, NS, 1024))
        m["st_im"] = np.ascontiguousarray(si[:, b0:b1].reshape(DEPTH, NS, 1024))
        m["st_sc"] = np.ascontiguousarray(ssc[:, b0:b1].reshape(DEPTH, NS * 2, C))
        m["st_cf"] = np.ascontiguousarray(scf[:, b0:b1].reshape(DEPTH, NS * 30, C))
        in_maps.append(m)
    res = run_bass_kernel_spmd(nc, in_maps, core_ids=list(range(NCORES)))
    R = res.results
    cat1 = lambda name, shp: np.concatenate([r[name].reshape(shp) for r in R], axis=1)
    stk1 = lambda name, shp: np.stack([r[name].reshape(shp) for r in R], axis=1)
    y_prompt = np.stack([r["y"][:TP] for r in R], axis=0)
    y_sample = np.concatenate([r["y"][TP:].reshape(NS, LS, D) for r in R], axis=0)
    outs = (y_prompt, y_sample,
            stk1("pool_p", (DEPTH, 15, C)), cat1("pool_s", (DEPTH, NS, 15, C)),
            stk1("re_p", (DEPTH, 16, 64)), cat1("re_s", (DEPTH, NS, 16, 64)),
            stk1("im_p", (DEPTH, 16, 64)), cat1("im_s", (DEPTH, NS, 16, 64)),
            stk1("sc_p", (DEPTH, 2, C)), cat1("sc_s", (DEPTH, NS, 2, C)),
            stk1("cf_p", (DEPTH, 30, C)), cat1("cf_s", (DEPTH, NS, 30, C)))
    return tuple(np.ascontiguousarray(o.astype(np.float32)) for o in outs)
```
